# Optimizing a Trainium2 kernel written in Bass

```python
import jax, jax.numpy as jnp
from jax import lax
import numpy as np

D_MODEL = 1024
BATCH = 4
SEQ = 4096
DEPTH = 1
DEC_BATCH = 128
DEC_SEQ = 8
PAST_LEN = 2048
PAGE_SIZE = 128

N_HEADS = 8
N_KV_HEADS = 2
GROUP = N_HEADS // N_KV_HEADS
HEAD_DIM = D_MODEL // N_HEADS
N_IDX_HEADS = 8
IDX_DIM = 64
IDX_SCALE = (N_IDX_HEADS * IDX_DIM) ** -0.5
TOPK_MAX = 256
D_CONV = D_MODEL
CONV_WIDTH = 3
D_FF = -(-8 * D_MODEL // (3 * 256)) * 256
D_PLE = 256
ROPE_THETA = 10000.0
Q_BLOCK = 128
EPS = 1e-6
IN_SIZES = (N_HEADS * HEAD_DIM, N_KV_HEADS * HEAD_DIM, N_KV_HEADS * HEAD_DIM,
            N_IDX_HEADS * IDX_DIM, IDX_DIM, N_IDX_HEADS,
            D_CONV, D_CONV, D_CONV, D_MODEL, D_MODEL)
D_IN_TOTAL = sum(IN_SIZES)

kernel_name = "hybrid_dsa_shortconv_decoder_step"


def rmsnorm(x, g):
    xf = x.astype(jnp.float32)
    var = jnp.mean(xf * xf, axis=-1, keepdims=True)
    return (xf * lax.rsqrt(var + EPS)).astype(x.dtype) * g


def rope(x, pos):
    half = x.shape[-1] // 2
    freqs = ROPE_THETA ** (-jnp.arange(half, dtype=jnp.float32) / half)
    ang = pos.astype(jnp.float32)[:, None] * freqs[None, :]
    cos = jnp.cos(ang)[:, None, :]
    sin = jnp.sin(ang)[:, None, :]
    xf = x.astype(jnp.float32)
    x1, x2 = xf[..., :half], xf[..., half:]
    return jnp.concatenate([x1 * cos - x2 * sin, x2 * cos + x1 * sin], axis=-1).astype(x.dtype)


def mix_inputs(h, w_in, pos):
    z = h @ w_in
    offs = []
    acc = 0
    for s in IN_SIZES[:-1]:
        acc += s
        offs.append(acc)
    q, k, v, qi, ki, wi, bg, cg, xc, ga, gb = jnp.split(z, offs, axis=-1)
    B, T = h.shape[:2]
    q = rope(q.reshape(B, T, N_HEADS, HEAD_DIM), pos)
    k = rope(k.reshape(B, T, N_KV_HEADS, HEAD_DIM), pos)
    v = v.reshape(B, T, N_KV_HEADS, HEAD_DIM)
    qi = rope(qi.reshape(B, T, N_IDX_HEADS, IDX_DIM), pos)
    ki = rope(ki[:, :, None, :], pos)[:, :, 0, :]
    return q, k, v, qi, ki, wi, bg, cg, xc, ga, gb


def select_attend(q, qi, wi, q_pos, k_all, v_all, ki_all):
    B, Tq = q.shape[:2]
    L = k_all.shape[1]
    n_sel = min(TOPK_MAX, L // 4)
    causal = jnp.arange(L)[None, :] <= q_pos[:, None]
    dots = jnp.einsum('bthd,bld->bthl', qi.astype(jnp.float32), ki_all.astype(jnp.float32))
    score = jnp.einsum('bth,bthl->btl', wi.astype(jnp.float32), jax.nn.relu(dots)) * IDX_SCALE
    score = jnp.where(causal[None], score, -jnp.inf)
    _, idx = lax.top_k(score, n_sel)
    valid = idx <= q_pos[None, :, None]
    gather = jax.vmap(lambda kb, ib: kb[ib])
    k_sel = gather(k_all, idx)
    v_sel = gather(v_all, idx)
    qg = q.reshape(B, Tq, N_KV_HEADS, GROUP, HEAD_DIM)
    s = jnp.einsum('btkgd,btjkd->btkgj', qg, k_sel).astype(jnp.float32) * (HEAD_DIM ** -0.5)
    s = jnp.where(valid[:, :, None, None, :], s, -jnp.inf)
    p = jax.nn.softmax(s, axis=-1).astype(v_sel.dtype)
    o = jnp.einsum('btkgj,btjkd->btkgd', p, v_sel)
    return o.reshape(B, Tq, N_HEADS * HEAD_DIM)


def prompt_attention(q, qi, wi, k, v, ki):
    B, T = q.shape[:2]
    nb = T // Q_BLOCK

    def to_blocks(a):
        return jnp.moveaxis(a.reshape(B, nb, Q_BLOCK, *a.shape[2:]), 1, 0)

    def blk(args):
        qb, qib, wib, start = args
        pos = start + jnp.arange(Q_BLOCK)
        return select_attend(qb, qib, wib, pos, k, v, ki)

    starts = jnp.arange(nb) * Q_BLOCK
    o = lax.map(blk, (to_blocks(q), to_blocks(qi), to_blocks(wi), starts))
    return jnp.moveaxis(o, 0, 1).reshape(B, T, N_HEADS * HEAD_DIM)


def short_conv(u, prev, w):
    T = u.shape[1]
    up = jnp.concatenate([prev, u], axis=1)
    out = w[0] * up[:, 0:T]
    for j in range(1, CONV_WIDTH):
        out = out + w[j] * up[:, j:j + T]
    return out, up[:, -(CONV_WIDTH - 1):]


def merge_out(o_attn, conv_out, bg, ga, gb, w_out):
    merged = jax.nn.sigmoid(ga) * o_attn + jax.nn.sigmoid(gb) * (bg * conv_out)
    return merged @ w_out


def ffn_and_ple(x, p, norm_ffn, w_gate_up, w_down, norm_ple, w_ple, w_ple_gate):
    h = rmsnorm(x, norm_ffn)
    gu = h @ w_gate_up
    g, u = gu[..., :D_FF], gu[..., D_FF:]
    x = x + (jax.nn.silu(g) * u) @ w_down
    gate = jax.nn.sigmoid(rmsnorm(x, norm_ple) @ w_ple_gate)
    return x + (p @ w_ple) * gate


def setup_inputs(seed: int = 0) -> dict:
    key = jax.random.key(seed)
    ks = jax.random.split(key, 24)
    f32 = jnp.float32
    n_pages = PAST_LEN // PAGE_SIZE
    n_used = DEC_BATCH * n_pages
    n_phys = n_used + -(-n_used // 4)
    nrm = lambda k, shape, s=1.0: jax.random.normal(k, shape, f32) * s
    page_table = jax.random.permutation(ks[0], n_phys)[:n_used].reshape(DEC_BATCH, n_pages).astype(jnp.int32)
    return {
        "x_prompt": nrm(ks[1], (BATCH, SEQ, D_MODEL)),
        "x_sample": nrm(ks[2], (DEC_BATCH, DEC_SEQ, D_MODEL)),
        "cache_k": nrm(ks[3], (DEPTH, n_phys, PAGE_SIZE, N_KV_HEADS, HEAD_DIM)),
        "cache_v": nrm(ks[4], (DEPTH, n_phys, PAGE_SIZE, N_KV_HEADS, HEAD_DIM)),
        "cache_kidx": nrm(ks[5], (DEPTH, n_phys, PAGE_SIZE, IDX_DIM)),
        "state_conv": nrm(ks[6], (DEPTH, DEC_BATCH, CONV_WIDTH - 1, D_CONV)),
        "page_table": page_table,
        "p_prompt": nrm(ks[7], (DEPTH, BATCH, SEQ, D_PLE)),
        "p_sample": nrm(ks[8], (DEPTH, DEC_BATCH, DEC_SEQ, D_PLE)),
        "norm_mix": 1.0 + nrm(ks[9], (DEPTH, D_MODEL), 0.01),
        "w_in": nrm(ks[10], (DEPTH, D_MODEL, D_IN_TOTAL), D_MODEL ** -0.5),
        "conv_w": nrm(ks[11], (DEPTH, CONV_WIDTH, D_CONV), CONV_WIDTH ** -0.5),
        "w_out": nrm(ks[12], (DEPTH, D_MODEL, D_MODEL), D_MODEL ** -0.5),
        "norm_ffn": 1.0 + nrm(ks[13], (DEPTH, D_MODEL), 0.01),
        "w_gate_up": nrm(ks[14], (DEPTH, D_MODEL, 2 * D_FF), D_MODEL ** -0.5),
        "w_down": nrm(ks[15], (DEPTH, D_FF, D_MODEL), D_FF ** -0.5),
        "norm_ple": 1.0 + nrm(ks[16], (DEPTH, D_MODEL), 0.01),
        "w_ple": nrm(ks[17], (DEPTH, D_PLE, D_MODEL), D_PLE ** -0.5),
        "w_ple_gate": nrm(ks[18], (DEPTH, D_MODEL, D_MODEL), D_MODEL ** -0.5),
        "norm_final": 1.0 + nrm(ks[19], (D_MODEL,), 0.01),
    }


def reference(x_prompt, x_sample, cache_k, cache_v, cache_kidx, state_conv, page_table,
              p_prompt, p_sample, norm_mix, w_in, conv_w, w_out, norm_ffn, w_gate_up, w_down,
              norm_ple, w_ple, w_ple_gate, norm_final):
    Bp, Tp = x_prompt.shape[:2]
    Bs, Ts = x_sample.shape[:2]
    past_len = page_table.shape[1] * PAGE_SIZE
    pos_p = jnp.arange(Tp)
    pos_s = past_len + jnp.arange(Ts)
    xp, xs = x_prompt, x_sample
    kp_l, vp_l, kip_l, cp_l = [], [], [], []
    ks_l, vs_l, kis_l, cs_l = [], [], [], []
    for l in range(DEPTH):
        h = rmsnorm(xp, norm_mix[l])
        q, k, v, qi, ki, wi, bg, cg, xc, ga, gb = mix_inputs(h, w_in[l], pos_p)
        o_a = prompt_attention(q, qi, wi, k, v, ki)
        conv_prev = jnp.zeros((Bp, CONV_WIDTH - 1, D_CONV), xp.dtype)
        conv_o, conv_new = short_conv(cg * xc, conv_prev, conv_w[l])
        xp = xp + merge_out(o_a, conv_o, bg, ga, gb, w_out[l])
        xp = ffn_and_ple(xp, p_prompt[l], norm_ffn[l], w_gate_up[l], w_down[l],
                         norm_ple[l], w_ple[l], w_ple_gate[l])
        kp_l.append(k); vp_l.append(v); kip_l.append(ki); cp_l.append(conv_new)

        h = rmsnorm(xs, norm_mix[l])
        q, k, v, qi, ki, wi, bg, cg, xc, ga, gb = mix_inputs(h, w_in[l], pos_s)
        past_k = cache_k[l][page_table].reshape(Bs, past_len, N_KV_HEADS, HEAD_DIM)
        past_v = cache_v[l][page_table].reshape(Bs, past_len, N_KV_HEADS, HEAD_DIM)
        past_ki = cache_kidx[l][page_table].reshape(Bs, past_len, IDX_DIM)
        k_all = jnp.concatenate([past_k, k], axis=1)
        v_all = jnp.concatenate([past_v, v], axis=1)
        ki_all = jnp.concatenate([past_ki, ki], axis=1)
        o_a = select_attend(q, qi, wi, pos_s, k_all, v_all, ki_all)
        conv_o, conv_new = short_conv(cg * xc, state_conv[l], conv_w[l])
        xs = xs + merge_out(o_a, conv_o, bg, ga, gb, w_out[l])
        xs = ffn_and_ple(xs, p_sample[l], norm_ffn[l], w_gate_up[l], w_down[l],
                         norm_ple[l], w_ple[l], w_ple_gate[l])
        ks_l.append(k); vs_l.append(v); kis_l.append(ki); cs_l.append(conv_new)

    y_prompt = rmsnorm(xp, norm_final)
    y_sample = rmsnorm(xs, norm_final)
    new_k_prompt = jnp.stack(kp_l)
    new_v_prompt = jnp.stack(vp_l)
    new_kidx_prompt = jnp.stack(kip_l)
    new_conv_prompt = jnp.stack(cp_l)
    new_k_sample = jnp.stack(ks_l)
    new_v_sample = jnp.stack(vs_l)
    new_kidx_sample = jnp.stack(kis_l)
    new_conv_sample = jnp.stack(cs_l)
    return (y_prompt, y_sample, new_k_prompt, new_v_prompt, new_kidx_prompt, new_conv_prompt,
            new_k_sample, new_v_sample, new_kidx_sample, new_conv_sample)
```

```python
import numpy as np
import concourse.bass as bass
import concourse.mybir as mybir
from concourse.bass_utils import run_bass_kernel_spmd

F32 = mybir.dt.float32
BF16 = mybir.dt.bfloat16
I32 = mybir.dt.int32
AF = mybir.ActivationFunctionType
ALU = mybir.AluOpType
AX = mybir.AxisListType

D = 1024
SEQ = 4096
NTILE = 32
DFF = 2816
NFF = 22
DIN = 7240
NPHYS = 2560
BIG = 1.0e30
EPS = 1e-6
NSEL = 256
NBIS = 12
OWN_A = [0, 3, 4, 7]
OWN_B = [1, 2, 5, 6]
O_Q, O_K, O_V, O_QI, O_KI, O_WI, O_BG, O_CG, O_XC, O_GA, O_GB = 0, 1024, 1280, 1536, 2048, 2112, 2120, 3144, 4168, 5192, 6216


class Builder:
    def __init__(self, nc):
        self.nc = nc
        self.eng = {}
        for name, h in (("pe", nc.tensor), ("act", nc.scalar), ("dve", nc.vector), ("pool", nc.gpsimd), ("sp", nc.sync)):
            self.eng[name] = dict(h=h, sem=nc.semaphore("sem_" + name).__enter__(), cnt=0, seen={}, name=name)
        self.rings = {}
        for q in ("sp", "pool"):
            self.rings[q] = dict(sems=[nc.semaphore("dq_%s_%d" % (q, i)).__enter__() for i in range(8)], n=0)
        self.writer = {}
        self.readers = {}
        self.out_events = []

    def _need(self, e, ev, same_ok):
        if ev is None:
            return
        sem, val, owner, semname = ev
        if owner == e["name"] and same_ok:
            return
        if e["seen"].get(semname, 0) >= val:
            return
        e["h"].wait_ge(sem, val)
        e["seen"][semname] = val

    def _deps(self, e, reads, writes, pe_acc=False):
        for r in reads:
            self._need(e, self.writer.get(r), False)
        for w in writes:
            self._need(e, self.writer.get(w), True)
            for ev in self.readers.get(w, {}).values():
                self._need(e, ev, True)

    def _commit(self, ev, reads, writes):
        for r in reads:
            self.readers.setdefault(r, {})[ev[2] + ev[3]] = ev
        for w in writes:
            self.writer[w] = ev
            self.readers[w] = {}

    def op(self, engname, fn, reads, writes):
        e = self.eng[engname]
        self._deps(e, reads, writes)
        inst = fn(e["h"])
        e["cnt"] += 1
        inst.then_inc(e["sem"], 1)
        ev = (e["sem"], e["cnt"], engname, "E" + engname)
        self._commit(ev, reads, writes)
        return ev

    def dma(self, q, out, in_, reads, writes, indirect=None, is_out=False):
        e = self.eng[q]
        ring = self.rings[q]
        n = ring["n"]
        slot = n % 8
        sem = ring["sems"][slot]
        semname = "R%s%d" % (q, slot)
        self._deps(e, reads, writes)
        if n >= 8:
            need = 16 * (n // 8)
            if e["seen"].get(semname, 0) < need:
                e["h"].wait_ge(sem, need)
                e["seen"][semname] = need
        if indirect is not None:
            inst = e["h"].indirect_dma_start(out=out, out_offset=None, in_=in_,
                                             in_offset=bass.IndirectOffsetOnAxis(ap=indirect, axis=0))
        else:
            inst = e["h"].dma_start(out=out, in_=in_)
        inst.then_inc(sem, 16)
        ring["n"] = n + 1
        ev = (sem, 16 * (n // 8 + 1), "dma" + q, semname)
        self._commit(ev, reads, writes)
        if is_out:
            self.out_events.append(ev)
        return ev

    def finish(self):
        e = self.eng["sp"]
        for ev in self.out_events:
            self._need(e, ev, False)
        for name in ("pe", "act", "dve", "pool"):
            o = self.eng[name]
            if o["cnt"] > 0:
                self._need(e, (o["sem"], o["cnt"], name, "E" + name), False)


def build_program(debug=False):
    nc = bass.Bass("TRN2", target_bir_lowering=False)
    B = Builder(nc)

    def din(name, shape, dt=F32):
        return nc.dram_tensor(name, shape, dt, kind="ExternalInput").ap()

    def dout(name, shape, dt=F32):
        return nc.dram_tensor(name, shape, dt, kind="ExternalOutput").ap()

    x_seq = din("x_seq", [SEQ, D])
    x_own = din("x_own", [2048, D])
    x_halo = din("x_halo", [4, 128, D])
    p_own = din("p_own", [2048, 256])
    x_smp = din("x_smp", [128, D])
    p_smp = din("p_smp", [128, 256])
    st_conv = din("st_conv", [32, D])
    ptab = din("ptab", [128, 32], I32)
    ck = din("ck", [NPHYS * 16, 2048])
    cv = din("cv", [NPHYS * 16, 2048])
    cki = din("cki", [NPHYS * 16, 512])
    w_in = din("w_in", [D, DIN])
    w_out = din("w_out", [D, D])
    w_gu = din("w_gu", [D, 2 * DFF])
    w_dn = din("w_dn", [DFF, D])
    w_pg = din("w_pg", [D, D])
    w_ple = din("w_ple", [256, D])
    gains = din("gains", [4, 128, D])
    convw = din("convw", [128, 8, 3])
    ident_d = din("ident", [128, 128])
    rope_seq = din("rope_seq", [NTILE, 128, 192])
    rope_own = din("rope_own", [17, 128, 192])
    amask_d = din("amask", [17, 128, 640])
    piota_d = din("piota", [128, 1])

    y_p = dout("y_p", [2048, D])
    y_s = dout("y_s", [128, D])
    nk_p = dout("nk_p", [SEQ, 256])
    nv_p = dout("nv_p", [SEQ, 256])
    nki_p = dout("nki_p", [SEQ, 64])
    nconv_p = dout("nconv_p", [2, D])
    nk_s = dout("nk_s", [128, 256])
    nv_s = dout("nv_s", [128, 256])
    nki_s = dout("nki_s", [128, 64])
    nconv_s = dout("nconv_s", [32, D])

    def sb(name, shape, dt):
        return nc.sbuf_tensor(name, shape, dt).__enter__()

    banks = [nc.psum_tensor("bank%d" % i, [128, 512], F32).__enter__() for i in range(8)]
    banksb = [b.bitcast(BF16) for b in banks]

    def PS(i):
        return "ps%d" % i

    KT = sb("KT", [128, 2, SEQ], BF16)
    VA = sb("VA", [128, 32, 2, 129], BF16)
    kiT = sb("kiT", [128, SEQ], BF16)
    kTn = sb("kTn", [128, 2, 128], BF16)
    identb = sb("identb", [128, 128], BF16)
    ident32 = sb("ident32_sb", [128, 128], F32)
    G = sb("G", [128, D], F32)
    cw = sb("cw", [128, 8, 3], F32)
    epsT = sb("epsT", [128, 1], F32)
    halfT = sb("halfT", [128, 1], F32)
    onesb = sb("onesb", [128, 128], BF16)
    xres = sb("xres", [128, 4, D], F32)
    hn = sb("hn", [128, D], BF16)
    hT = sb("hT", [128, 8, 512], BF16)
    hTh = sb("hTh", [128, 8, 128], BF16)
    sm = sb("sm", [128, 16], F32)
    rope_t = sb("rope_t", [128, 192], F32)
    zs = sb("zs", [128, D], F32)
    t1 = sb("t1", [128, 512], F32)
    t2 = sb("t2", [128, 512], F32)
    T3 = t1
    SG = t2
    kout = sb("kout", [128, 256], F32)
    kiout = sb("kiout", [128, 64], F32)
    krb = sb("krb", [128, 256], BF16)
    kib2 = sb("kib2", [128, 128], BF16)
    qrb = sb("qrb", [128, D], BF16)
    qirb = sb("qirb", [128, 512], BF16)
    qT2 = [sb("qT%d" % i, [128, 8, 128], BF16) for i in range(2)]
    qT = qT2[0]
    qiT = sb("qiT", [128, 4, 128], BF16)
    wi = sb("wi", [128, 8], F32)
    absw = sb("absw", [128, 8], F32)
    sgnw = sb("sgnw", [128, 8], F32)
    scores = sb("scores", [128, SEQ], F32)
    maskT = sb("maskT", [128, 32, 128], BF16)
    maskc = sb("maskc", [128, 1024], BF16)
    junk = sb("junk", [128, SEQ], mybir.dt.uint8)
    on_bt = sb("on_bt", [128, 8, 128], BF16)
    den8 = sb("den8", [128, 16], F32)
    pw = sb("pw", [128, 32], F32)
    esum = sb("esum", [128, 64], F32)
    ones32 = sb("ones32", [128, 128], F32)
    hk = sb("hk", [128, 64], F32)
    rbuf = [sb("rbuf%d" % i, [128, 512], F32) for i in range(2)]
    ebuf = [sb("ebuf%d" % i, [128, 512], BF16) for i in range(3)]
    amask = sb("amask_t", [128, 640], F32)
    oT = sb("oT", [128, 8, 512], BF16)
    WS = [sb("WS%d" % i, [128, 5120], BF16) for i in range(3)]
    CGC = sb("CGC", [128, 2, 514], F32)
    U = sb("U", [128, 2, 514], F32)
    ulast = sb("ulast", [128, 8, 32], F32)
    actT = sb("actT", [128, 11, 512], BF16)
    pt_f = sb("pt_f", [128, 256], F32)
    pb = sb("pb", [128, 256], BF16)
    pT = sb("pT", [128, 2, 512], BF16)
    ptab_i = sb("ptab_i", [128, 32], I32)
    piota = sb("piota_sb", [128, 1], F32)
    gidx = sb("gidx", [128, 32], I32)
    eS = sb("eS", [128, 17, 64], BF16)
    rden = sb("rden", [128, 64], F32)

    ws_n = [0]
    xt = sb("xt", [128, D], F32)
    yt = zs
    gcur = [-1]

    def V(fn, reads, writes):
        return B.op("dve", fn, reads, writes)

    def A(fn, reads, writes):
        return B.op("act", fn, reads, writes)

    def PE(fn, reads, writes):
        return B.op("pe", fn, reads, writes)

    def LD(out, in_, writes, reads=()):
        return B.dma("sp", out, in_, list(reads), list(writes))

    def LDC(out, in_, writes, reads=()):
        return B.dma("pool", out, in_, list(reads), list(writes))

    def ST(out, in_, reads):
        return B.dma("sp", out, in_, list(reads), [], is_out=True)

    LDC(identb[:], ident_d, ["identb"])
    LD(ident32[:], ident_d, ["ident32"])
    LD(cw[:], convw, ["cw"])
    LD(piota[:], piota_d, ["piota"])
    V(lambda e: e.memset(epsT[:], EPS), [], ["epsT"])
    V(lambda e: e.memset(halfT[:], 0.5), [], ["halfT"])
    V(lambda e: e.memset(sm[:, 11:12], -30000.0), [], ["negb"])
    V(lambda e: e.memset(ones32[:], 1.0), [], ["ones32"])
    V(lambda e: e.memset(onesb[:], 1.0), [], ["onesb"])
    V(lambda e: e.memset(VA[:, :, :, 128:129], 1.0), [], ["VA"])
    for k in range(NBIS + 1):
        V(lambda e, k=k: e.memset(pw[:, k:k + 1], 0.5 ** (k + 1)), [], ["pw"])

    def norm_tile(x_ap, xkey, gi, out_hn=hn, okey="hn", sc=0):
        if gcur[0] != gi:
            LD(G[:], gains[gi], ["G"])
            gcur[0] = gi
        c0 = 13 if sc else 0
        k0, k1, k2 = "sm%d" % c0, "sm%d" % (c0 + 1), "sm%d" % (c0 + 2)
        A(lambda e: e.activation(out=out_hn[:], in_=x_ap, func=AF.Square, accum_out=sm[:, c0:c0 + 1]), [xkey], [okey, k0])
        A(lambda e: e.activation(out=sm[:, c0 + 1:c0 + 2], in_=sm[:, c0:c0 + 1], func=AF.Sqrt, scale=1.0 / D, bias=epsT[:, 0:1]), [k0, "epsT"], [k1])
        V(lambda e: e.reciprocal(out=sm[:, c0 + 2:c0 + 3], in_=sm[:, c0 + 1:c0 + 2]), [k1], [k2])
        V(lambda e: e.scalar_tensor_tensor(out=out_hn[:], in0=x_ap, scalar=sm[:, c0 + 2:c0 + 3], in1=G[:], op0=ALU.mult, op1=ALU.mult),
          [xkey, k2, "G", okey], [okey])

    def transpose_to(dst_ap, dkey, src, skey, nblk, bank=2):
        pb_ = banksb[bank]
        for k in range(nblk):
            PE(lambda e, k=k: e.transpose(out=pb_[:, k * 128:(k + 1) * 128], in_=src[:, k * 128:(k + 1) * 128], identity=identb[:]),
               [skey, "identb"], [PS(bank)])
        A(lambda e: e.copy(out=dst_ap, in_=pb_[:, 0:nblk * 128].rearrange("p (k c) -> p k c", k=nblk)), [PS(bank)], [dkey])

    def norm_T_tiles(items):
        bufs = [(hn, "hn"), (qrb, "qrb")]

        def n_(i):
            x_ap, xkey, gi, _, _ = items[i]
            b_, k_ = bufs[i % 2]
            norm_tile(x_ap, xkey, gi, out_hn=b_, okey=k_, sc=i % 2)

        def t_(i):
            _, _, _, dst, dkey = items[i]
            b_, k_ = bufs[i % 2]
            transpose_to(dst, dkey, b_, k_, 8, bank=(2 if i % 2 == 0 else 7))

        n_(0)
        for i in range(len(items)):
            if i + 1 < len(items):
                n_(i + 1)
            t_(i)

    def rope(src, skey, H, Dh, cos, sin, dst, dkey):
        half = Dh // 2
        s3 = src.rearrange("p (h d) -> p h d", h=H)
        d3 = dst.rearrange("p (h d) -> p h d", h=H)
        x1, x2 = s3[:, :, 0:half], s3[:, :, half:Dh]
        cb = cos.unsqueeze(1).to_broadcast([128, H, half])
        sbb = sin.unsqueeze(1).to_broadcast([128, H, half])
        a1 = t1[:, 0:H * half].rearrange("p (h d) -> p h d", h=H)
        a2 = t2[:, 0:H * half].rearrange("p (h d) -> p h d", h=H)
        V(lambda e: e.tensor_tensor(out=a1, in0=x1, in1=cb, op=ALU.mult), [skey, "rope_t"], ["t1"])
        V(lambda e: e.tensor_tensor(out=a2, in0=x2, in1=sbb, op=ALU.mult), [skey, "rope_t"], ["t2"])
        V(lambda e: e.tensor_tensor(out=d3[:, :, 0:half], in0=a1, in1=a2, op=ALU.subtract), ["t1", "t2"], [dkey])
        V(lambda e: e.tensor_tensor(out=a1, in0=x2, in1=cb, op=ALU.mult), [skey, "rope_t"], ["t1"])
        V(lambda e: e.tensor_tensor(out=a2, in0=x1, in1=sbb, op=ALU.mult), [skey, "rope_t"], ["t2"])
        V(lambda e: e.tensor_tensor(out=d3[:, :, half:Dh], in0=a1, in1=a2, op=ALU.add), ["t1", "t2"], [dkey])

    def load_w(src_list):
        i = ws_n[0] % 3
        ws_n[0] += 1
        off = 0
        views = []
        for (ap, nk, ncol) in src_list:
            v = WS[i][:, off:off + nk * ncol].rearrange("p (k n) -> p k n", k=nk)
            LDC(v, ap.rearrange("(k p) n -> p k n", p=128), ["WS%d" % i])
            views.append(v)
            off += nk * ncol
        assert off <= 5120
        return "WS%d" % i, views

    wkv = {}

    def load_wkv():
        k, (a_, b_) = load_w([(w_in[:, O_K:O_K + 512], 8, 512), (w_in[:, O_KI:O_KI + 64], 8, 64)])
        wkv["key"], wkv["kv"], wkv["ki"] = k, a_, b_

    BS = [dict(zs=zs[:, :], zk="zs", kout=kout[:, :], kok="kout", kiout=kiout[:, :], kik="kiout", krb=krb[:, :], krk="krb", kib2=kib2[:, :], k2k="kib2",
               rt=rope_t[:, :], rtk="rope_t", b0=0, b1=1, b2=2),
          dict(zs=scores[:, 0:1024], zk="scores", kout=scores[:, 1024:1280], kok="scoresB", kiout=scores[:, 1280:1344], kik="scoresC",
               krb=maskc[:, 0:256], krk="maskc", kib2=maskc[:, 256:384], k2k="maskcB", rt=amask[:, 0:192], rtk="amask", b0=3, b1=4, b2=5)]

    def rope2(src, skey, H, Dh, cos, sin, rtk, dst, dkey):
        half = Dh // 2
        s3 = src.rearrange("p (h d) -> p h d", h=H)
        d3 = dst.rearrange("p (h d) -> p h d", h=H)
        x1, x2 = s3[:, :, 0:half], s3[:, :, half:Dh]
        cb = cos.unsqueeze(1).to_broadcast([128, H, half])
        sbb = sin.unsqueeze(1).to_broadcast([128, H, half])
        a1 = t1[:, 0:H * half].rearrange("p (h d) -> p h d", h=H)
        a2 = t2[:, 0:H * half].rearrange("p (h d) -> p h d", h=H)
        V(lambda e: e.tensor_tensor(out=a1, in0=x1, in1=cb, op=ALU.mult), [skey, rtk], ["t1"])
        V(lambda e: e.tensor_tensor(out=a2, in0=x2, in1=sbb, op=ALU.mult), [skey, rtk], ["t2"])
        V(lambda e: e.tensor_tensor(out=d3[:, :, 0:half], in0=a1, in1=a2, op=ALU.subtract), ["t1", "t2"], [dkey])
        V(lambda e: e.tensor_tensor(out=a1, in0=x2, in1=cb, op=ALU.mult), [skey, rtk], ["t1"])
        V(lambda e: e.tensor_tensor(out=a2, in0=x1, in1=sbb, op=ALU.mult), [skey, rtk], ["t2"])
        V(lambda e: e.tensor_tensor(out=d3[:, :, half:Dh], in0=a1, in1=a2, op=ALU.add), ["t1", "t2"], [dkey])

    def kv_from_h_gen(hT_ap, hkey, rope_src_ap, k_dst, v_dst, ki_dst, va_tile, kt_dst, kt_key, kit_cols, bs=0):
        wkv_v, wki_v, wkv_key = wkv["kv"], wkv["ki"], wkv["key"]
        S_ = BS[bs]
        z, zk, ko, kok, kio, kik = S_["zs"], S_["zk"], S_["kout"], S_["kok"], S_["kiout"], S_["kik"]
        kr, krk, k2, k2k, rt, rtk = S_["krb"], S_["krk"], S_["kib2"], S_["k2k"], S_["rt"], S_["rtk"]
        b0, b1, b2 = S_["b0"], S_["b1"], S_["b2"]
        LD(rt, rope_src_ap, [rtk])
        for kc in range(8):
            PE(lambda e, kc=kc: e.matmul(banks[b0][:, :], lhsT=hT_ap[:, kc, :], rhs=wkv_v[:, kc, :], start=(kc == 0), stop=(kc == 7)),
               [hkey, wkv_key], [PS(b0)])
        for kc in range(8):
            PE(lambda e, kc=kc: e.matmul(banks[b1][:, 0:64], lhsT=hT_ap[:, kc, :], rhs=wki_v[:, kc, :], start=(kc == 0), stop=(kc == 7)),
               [hkey, wkv_key], [PS(b1)])
        A(lambda e: e.copy(out=z[:, 0:512], in_=banks[b0][:, :]), [PS(b0)], [zk])
        A(lambda e: e.copy(out=z[:, 512:576], in_=banks[b1][:, 0:64]), [PS(b1)], [zk])
        yield
        rope2(z[:, 0:256], zk, 2, 128, rt[:, 0:64], rt[:, 64:128], rtk, ko, kok)
        rope2(z[:, 512:576], zk, 1, 64, rt[:, 128:160], rt[:, 160:192], rtk, kio, kik)
        ST(k_dst, ko, [kok])
        ST(v_dst, z[:, 256:512], [zk])
        ST(ki_dst, kio, [kik])
        A(lambda e: e.copy(out=kr, in_=ko), [kok], [krk])
        A(lambda e: e.copy(out=k2[:, 0:64], in_=kio), [kik], [k2k])
        A(lambda e: e.copy(out=k2[:, 64:128], in_=kio), [kik], [k2k])
        V(lambda e: e.tensor_copy(out=VA[:, va_tile, :, 0:128], in_=z[:, 256:512].rearrange("p (k d) -> p k d", k=2)), [zk], ["VA"])
        pb_ = banksb[b2]
        for kv in range(2):
            PE(lambda e, kv=kv: e.transpose(out=pb_[:, kv * 128:(kv + 1) * 128], in_=kr[:, kv * 128:(kv + 1) * 128], identity=identb[:]),
               [krk, "identb"], [PS(b2)])
        PE(lambda e: e.transpose(out=pb_[:, 256:384], in_=k2, identity=identb[:]), [k2k, "identb"], [PS(b2)])
        A(lambda e: e.copy(out=kt_dst, in_=pb_[:, 0:256].rearrange("p (k c) -> p k c", k=2)), [PS(b2)], [kt_key])
        A(lambda e: e.copy(out=kiT[:, kit_cols:kit_cols + 128], in_=pb_[:, 256:384]), [PS(b2)], ["kiT"])

    def kv_from_h(*a, **k):
        for _ in kv_from_h_gen(*a, **k):
            pass

    load_wkv()
    NPRE = 3
    pend_ = None
    for j in range(min(NPRE, NTILE)):
        LD(xres[:, j % 4, :], x_seq[j * 128:(j + 1) * 128, :], ["xres%d" % (j % 4)])
    for j in range(NTILE):
        bs = j % 2
        if j + NPRE < NTILE:
            LD(xres[:, (j + NPRE) % 4, :], x_seq[(j + NPRE) * 128:(j + NPRE + 1) * 128, :], ["xres%d" % ((j + NPRE) % 4)])
        hnb, hnk = (hn, "hn") if bs == 0 else (qrb, "qrb")
        hTb, hTk = (hTh[:, :, :], "hTh") if bs == 0 else (hT[:, :, 0:128], "hT")
        norm_tile(xres[:, j % 4, :], "xres%d" % (j % 4), 0, out_hn=hnb, okey=hnk, sc=bs)
        transpose_to(hTb, hTk, hnb, hnk, 8, bank=(2 if bs == 0 else 5))
        g_ = kv_from_h_gen(hTb, hTk, rope_seq[j], nk_p[j * 128:(j + 1) * 128, :], nv_p[j * 128:(j + 1) * 128, :],
                           nki_p[j * 128:(j + 1) * 128, :], j, KT[:, :, j * 128:(j + 1) * 128], "KT", j * 128, bs=bs)
        next(g_)
        if pend_ is not None:
            for _ in pend_:
                pass
        pend_ = g_
    for _ in pend_:
        pass
    V(lambda e: e.memset(sm[:, 12:13], 0.0), [], ["xres0", "xres1", "xres2", "xres3", "xres", "scores", "scoresB", "scoresC", "maskc", "maskcB", "amask", "qrb", "hT", "sm12"])

    maskflat = maskT[:, :, :].rearrange("p a b -> p (a b)")

    def indexer_chunk_epilogue(h, c0, w, bank):
        rb = rbuf[h % 2]
        rk = "rbuf%d" % (h % 2)
        A(lambda e: e.activation(out=rb[:, 0:w], in_=banks[bank][:, 0:w], func=AF.Relu, scale=absw[:, h:h + 1]),
          [PS(bank), "absw"], [rk])
        if h == 0:
            V(lambda e: e.tensor_scalar(out=scores[:, c0:c0 + w], in0=rb[:, 0:w], scalar1=sgnw[:, 0:1], scalar2=None, op0=ALU.mult),
              [rk, "sgnw"], ["scores"])
        else:
            V(lambda e: e.scalar_tensor_tensor(out=scores[:, c0:c0 + w], in0=rb[:, 0:w], scalar=sgnw[:, h:h + 1],
                                               in1=scores[:, c0:c0 + w], op0=ALU.mult, op1=ALU.add),
              [rk, "sgnw", "scores"], ["scores"])

    def score_bounds(L):
        lo, hi = sm[:, 4:5], sm[:, 5:6]
        V(lambda e: e.tensor_reduce(out=hi, in_=scores[:, 0:L], axis=AX.X, op=ALU.max), ["scores"], ["hi"])
        V(lambda e: e.tensor_reduce(out=lo, in_=scores[:, 0:L], axis=AX.X, op=ALU.min), ["scores"], ["lo"])
        V(lambda e: e.scalar_tensor_tensor(out=hi, in0=hi, scalar=2.0, in1=lo, op0=ALU.add, op1=ALU.subtract), ["hi", "lo"], ["hi"])
        V(lambda e: e.tensor_scalar(out=lo, in0=lo, scalar1=-1.0, scalar2=None, op0=ALU.add), ["lo"], ["lo"])
        V(lambda e: e.tensor_scalar(out=hk[:, 0:NBIS + 1], in0=pw[:, 0:NBIS + 1], scalar1=hi, scalar2=None, op0=ALU.mult), ["pw", "hi"], ["hk"])
        V(lambda e: e.tensor_scalar(out=hk[:, 32:32 + NBIS + 1], in0=hk[:, 0:NBIS + 1], scalar1=2.0, scalar2=None, op0=ALU.mult), ["hk"], ["hk"])

    def bisect(L):
        lo, mid, cnt, g = sm[:, 4:5], sm[:, 6:7], sm[:, 7:8], sm[:, 8:9]
        V(lambda e: e.tensor_tensor(out=mid, in0=lo, in1=hk[:, 0:1], op=ALU.add), ["lo", "hk"], ["mid"])
        for k in range(NBIS):
            V(lambda e: e.tensor_scalar(out=junk[:, 0:L], in0=scores[:, 0:L], scalar1=mid, scalar2=None, op0=ALU.is_ge, op1=ALU.add, accum_out=cnt),
              ["scores", "mid"], ["junk", "cnt"])
            V(lambda e, k=k: e.tensor_scalar(out=g, in0=cnt, scalar1=NSEL - 0.5, scalar2=hk[:, 32 + k + 1:32 + k + 2], op0=ALU.is_ge, op1=ALU.mult), ["cnt", "hk"], ["g"])
            V(lambda e, k=k: e.scalar_tensor_tensor(out=mid, in0=g, scalar=hk[:, k + 1:k + 2], in1=mid, op0=ALU.subtract, op1=ALU.add), ["g", "hk", "mid"], ["mid"])
        V(lambda e: e.tensor_tensor(out=lo, in0=mid, in1=hk[:, NBIS:NBIS + 1], op=ALU.subtract), ["mid", "hk"], ["lo"])

    def make_maskT(nkt, neg=False):
        lo = sm[:, 4:5]
        pb_ = banksb[2]
        for j0 in range(0, nkt, 8):
            n = min(8, nkt - j0)
            V(lambda e, j0=j0, n=n: e.tensor_scalar(out=maskc[:, 0:n * 128], in0=scores[:, j0 * 128:(j0 + n) * 128], scalar1=lo, scalar2=None, op0=ALU.is_ge),
              ["scores", "lo"], ["maskc"])
            for j in range(j0, j0 + n):
                PE(lambda e, j=j, j0=j0: e.transpose(out=pb_[:, (j - j0) * 128:(j - j0 + 1) * 128], in_=maskc[:, (j - j0) * 128:(j - j0 + 1) * 128], identity=identb[:]),
                   ["maskc", "identb"], [PS(2)])
            if neg:
                A(lambda e, j0=j0, n=n: e.activation(out=maskT[:, j0:j0 + n, :], in_=pb_[:, 0:n * 128].rearrange("p (k c) -> p k c", k=n),
                                                     func=AF.Identity, scale=30000.0, bias=sm[:, 11:12]), [PS(2), "negb"], ["maskT"])
            else:
                A(lambda e, j0=j0, n=n: e.copy(out=maskT[:, j0:j0 + n, :], in_=pb_[:, 0:n * 128].rearrange("p (k c) -> p k c", k=n)),
                  [PS(2)], ["maskT"])

    def topk_mask(L, nkt):
        bisect(L)
        make_maskT(nkt)

    wq_v = [None] * 4
    wq_key = [None] * 3

    def load_wq():
        for blk, off in enumerate((0, 512)):
            k, (v,) = load_w([(w_in[:, off:off + 512], 8, 512)])
            wq_v[blk] = v
            wq_key[blk] = k
        k, (v2, v3) = load_w([(w_in[:, O_QI:O_QI + 512], 8, 512), (w_in[:, O_WI:O_WI + 8], 8, 8)])
        wq_v[2], wq_v[3] = v2, v3
        wq_key[2] = k

    def q_proj_tile(hT_cols, rope_ap, qTd=None, qkey="qT0"):
        qTd = qT2[0] if qTd is None else qTd
        LD(rope_t[:], rope_ap, ["rope_t"])
        for blk in range(2):
            for kc in range(8):
                PE(lambda e, kc=kc, blk=blk: e.matmul(banks[blk][:, :], lhsT=hT[:, kc, hT_cols:hT_cols + 128], rhs=wq_v[blk][:, kc, :],
                                                      start=(kc == 0), stop=(kc == 7)), ["hT", wq_key[blk]], [PS(blk)])
            A(lambda e, blk=blk: e.copy(out=zs[:, blk * 512:(blk + 1) * 512], in_=banks[blk][:, :]), [PS(blk)], ["zs"])
        rope(zs[:, :], "zs", 8, 128, rope_t[:, 0:64], rope_t[:, 64:128], qrb[:], "qrb")
        transpose_to(qTd[:, :, :], qkey, qrb, "qrb", 8)
        for kc in range(8):
            PE(lambda e, kc=kc: e.matmul(banks[0][:, :], lhsT=hT[:, kc, hT_cols:hT_cols + 128], rhs=wq_v[2][:, kc, :], start=(kc == 0), stop=(kc == 7)),
               ["hT", wq_key[2]], [PS(0)])
        for kc in range(8):
            PE(lambda e, kc=kc: e.matmul(banks[1][:, 0:8], lhsT=hT[:, kc, hT_cols:hT_cols + 128], rhs=wq_v[3][:, kc, :], start=(kc == 0), stop=(kc == 7)),
               ["hT", wq_key[2]], [PS(1)])
        A(lambda e: e.copy(out=zs[:, 0:512], in_=banks[0][:, :]), [PS(0)], ["zs"])
        A(lambda e: e.copy(out=wi[:], in_=banks[1][:, 0:8]), [PS(1)], ["wi"])
        rope(zs[:, 0:512], "zs", 8, 64, rope_t[:, 128:160], rope_t[:, 160:192], qirb[:], "qirb")
        transpose_to(qiT[:, :, :], "qiT", qirb, "qirb", 4)
        A(lambda e: e.activation(out=absw[:], in_=wi[:], func=AF.Abs), ["wi"], ["absw"])
        A(lambda e: e.activation(out=sgnw[:], in_=wi[:], func=AF.Sign), ["wi"], ["sgnw"])

    o_raw = xt[:, :].rearrange("p (h d) -> p h d", h=8)

    IDXB = (2, 7)

    def gen_A(nkt):
        L = nkt * 128
        items = []
        for c0 in range(0, L, 512):
            w = min(512, L - c0)
            for h in range(8):
                items.append((c0, w, h))
        N = len(items)

        def mm(n):
            c0, w, h = items[n]
            bank = IDXB[n % 2]
            r0 = (h % 2) * 64
            PE(lambda e: e.matmul(banks[bank][:, 0:w], lhsT=qiT[r0:r0 + 64, h // 2, :], rhs=kiT[r0:r0 + 64, c0:c0 + w], start=True, stop=True),
               ["qiT", "kiT"], [PS(bank)])

        def relu(n):
            c0, w, h = items[n]
            bank = IDXB[n % 2]
            rb, rk = rbuf[n % 2], "rbuf%d" % (n % 2)
            A(lambda e: e.activation(out=rb[:, 0:w], in_=banks[bank][:, 0:w], func=AF.Relu, scale=absw[:, h:h + 1]), [PS(bank), "absw"], [rk])

        def acc(n):
            c0, w, h = items[n]
            rb, rk = rbuf[n % 2], "rbuf%d" % (n % 2)
            if h == 0:
                V(lambda e: e.tensor_scalar(out=scores[:, c0:c0 + w], in0=rb[:, 0:w], scalar1=sgnw[:, 0:1], scalar2=None, op0=ALU.mult), [rk, "sgnw"], ["scores"])
            else:
                V(lambda e: e.scalar_tensor_tensor(out=scores[:, c0:c0 + w], in0=rb[:, 0:w], scalar=sgnw[:, h:h + 1], in1=scores[:, c0:c0 + w],
                                                   op0=ALU.mult, op1=ALU.add), [rk, "sgnw", "scores"], ["scores"])

        for n in range(N + 2):
            if n < N:
                mm(n)
            if 0 <= n - 1 < N:
                relu(n - 1)
            if 0 <= n - 2 < N:
                acc(n - 2)
            yield
        score_bounds(L)
        V(lambda e: e.tensor_tensor(out=scores[:, L - 640:L], in0=scores[:, L - 640:L], in1=amask[:, :], op=ALU.add), ["scores", "amask"], ["scores"])
        yield
        lo, mid, cnt, g = sm[:, 4:5], sm[:, 6:7], sm[:, 7:8], sm[:, 8:9]
        V(lambda e: e.tensor_tensor(out=mid, in0=lo, in1=hk[:, 0:1], op=ALU.add), ["lo", "hk"], ["mid"])
        for k in range(NBIS):
            V(lambda e: e.tensor_scalar(out=junk[:, 0:L], in0=scores[:, 0:L], scalar1=mid, scalar2=None, op0=ALU.is_ge, op1=ALU.add, accum_out=cnt),
              ["scores", "mid"], ["junk", "cnt"])
            V(lambda e, k=k: e.tensor_scalar(out=g, in0=cnt, scalar1=NSEL - 0.5, scalar2=hk[:, 32 + k + 1:32 + k + 2], op0=ALU.is_ge, op1=ALU.mult), ["cnt", "hk"], ["g"])
            V(lambda e, k=k: e.scalar_tensor_tensor(out=mid, in0=g, scalar=hk[:, k + 1:k + 2], in1=mid, op0=ALU.subtract, op1=ALU.add), ["g", "hk", "mid"], ["mid"])
            yield
        V(lambda e: e.tensor_tensor(out=lo, in0=mid, in1=hk[:, NBIS:NBIS + 1], op=ALU.subtract), ["mid", "hk"], ["lo"])

    def n_items_A(nkt):
        return 8 * ((nkt * 128 + 511) // 512) + 3 + NBIS

    def stage_C(nkt, qTd, qkey, gen=None, gen_items=0):
        steps = [(kv, j) for kv in range(2) for j in range(nkt)]

        def emit_ST(i):
            kv, j = steps[i]
            bank = i % 2
            PE(lambda e, j=j, kv=kv, bank=bank: e.matmul(banks[bank][:, :], lhsT=KT[:, kv, j * 128:(j + 1) * 128],
                                                         rhs=qTd[:, kv * 4:(kv + 1) * 4, :], start=True, stop=False),
               ["KT", qkey], [PS(bank)])
            PE(lambda e, j=j, bank=bank: e.matmul(banks[bank][:, :].rearrange("p (g q) -> p g q", g=4), lhsT=identb[:],
                                                  rhs=maskT[:, j, :].unsqueeze(1).to_broadcast([128, 4, 128]), start=False, stop=True),
               ["identb", "maskT"], [PS(bank)])

        def emit_exp(i):
            bank = i % 2
            eb = ebuf[i % 3]
            ek = "ebuf%d" % (i % 3)
            A(lambda e: e.activation(out=eb[:], in_=banks[bank][:, :], func=AF.Exp, scale=128.0 ** -0.5), [PS(bank)], [ek])

        emit_ST(0)
        if len(steps) > 1:
            emit_ST(1)
        emit_exp(0)
        acc = 0.0
        rate = max(1.0, 4.5 * gen_items / float(len(steps))) if gen is not None else 0.0
        for i, (kv, j) in enumerate(steps):
            if i + 2 < len(steps):
                emit_ST(i + 2)
            if i + 1 < len(steps):
                emit_exp(i + 1)
            eb = ebuf[i % 3]
            ek = "ebuf%d" % (i % 3)
            if gen is not None:
                acc += rate
                while acc >= 1.0:
                    acc -= 1.0
                    next(gen, None)
            for g in range(4):
                PE(lambda e, j=j, kv=kv, g=g, eb=eb: e.matmul(banks[3 + g][:, 0:129], lhsT=eb[:, g * 128:(g + 1) * 128], rhs=VA[:, j, kv, :],
                                                              start=(j == 0), stop=(j == nkt - 1)), [ek, "VA"], [PS(3 + g)])
            if j == nkt - 1:
                for g in range(4):
                    hh = kv * 4 + g
                    A(lambda e, g=g, hh=hh: e.copy(out=o_raw[:, hh, :], in_=banks[3 + g][:, 0:128]), [PS(3 + g)], ["xt"])
                    A(lambda e, g=g, hh=hh: e.copy(out=den8[:, hh:hh + 1], in_=banks[3 + g][:, 128:129]), [PS(3 + g)], ["den8"])
        if gen is not None:
            for _ in gen:
                pass

    def stage_D(oT_cols):
        V(lambda e: e.reciprocal(out=den8[:, 8:16], in_=den8[:, 0:8]), ["den8"], ["rden8"])
        V(lambda e: e.tensor_tensor(out=on_bt[:, :, :], in0=o_raw, in1=den8[:, 8:16].unsqueeze(2).to_broadcast([128, 8, 128]), op=ALU.mult),
          ["xt", "rden8"], ["on_bt"])
        pb_ = banksb[2]
        for hh in range(8):
            PE(lambda e, hh=hh: e.transpose(out=pb_[:, hh * 128:(hh + 1) * 128], in_=on_bt[:, hh, :], identity=identb[:]), ["on_bt", "identb"], [PS(2)])
        A(lambda e: e.copy(out=oT[:, :, oT_cols:oT_cols + 128], in_=pb_[:, :].rearrange("p (k c) -> p k c", k=8)), [PS(2)], ["oT"])

    def dense_tail(NT, ntl, sample, x_dram_out, p_dram, slot_is_last):
        NB, WW = (16, 8) if sample else (1, NT)
        UW = NB * (WW + 2)

        def uview(buf, f):
            return buf[:, f, 0:UW].rearrange("p (b w) -> p b w", b=NB)

        def pview(bank):
            return banks[bank][:, 0:NT].rearrange("p (b w) -> p b w", b=NB)

        def v3(ap):
            return ap.rearrange("p (b w) -> p b w", b=NB)

        for qb in range(4):
            segs = [("cg", O_CG), ("xc", O_XC), ("bg", O_BG), ("gb", O_GB), ("ga", O_GA)]
            for si, (sname, soff) in enumerate(segs):
                wk, (wv,) = load_w([(w_in[:, soff + qb * 256: soff + qb * 256 + 256], 8, 256)])
                for f in range(2):
                    fc = qb * 2 + f
                    bank = f % 2
                    for kc in range(8):
                        PE(lambda e, kc=kc, f=f, bank=bank: e.matmul(banks[bank][:, 0:NT], lhsT=wv[:, kc, f * 128:(f + 1) * 128], rhs=hT[:, kc, 0:NT],
                                                                     start=(kc == 0), stop=(kc == 7)), ["hT", wk], [PS(bank)])
                    if sname in ("cg", "xc") and not sample:
                        for kc in range(8):
                            PE(lambda e, kc=kc, f=f: e.matmul(banks[7][:, 0:2], lhsT=wv[:, kc, f * 128:(f + 1) * 128], rhs=hTh[:, kc, 0:2],
                                                              start=(kc == 0), stop=(kc == 7)), ["hTh", wk], [PS(7)])
                    cv_ = uview(CGC, f)[:, :, 0:WW]
                    if sname == "cg":
                        A(lambda e, f=f, bank=bank: e.copy(out=uview(CGC, f)[:, :, 2:WW + 2], in_=pview(bank)), [PS(bank)], ["CGC"])
                        if not sample:
                            A(lambda e, f=f: e.copy(out=CGC[:, f, 0:2], in_=banks[7][:, 0:2]), [PS(7)], ["CGC"])
                    elif sname == "xc":
                        V(lambda e, f=f, bank=bank: e.tensor_tensor(out=uview(U, f)[:, :, 2:WW + 2], in0=pview(bank), in1=uview(CGC, f)[:, :, 2:WW + 2], op=ALU.mult),
                          [PS(bank), "CGC"], ["U"])
                        if not sample:
                            V(lambda e, f=f: e.tensor_tensor(out=U[:, f, 0:2], in0=banks[7][:, 0:2], in1=CGC[:, f, 0:2], op=ALU.mult), [PS(7), "CGC"], ["U"])
                        else:
                            V(lambda e, f=f, fc=fc: e.tensor_copy(out=uview(U, f)[:, :, 0:2], in_=ulast[:, fc, :].rearrange("p (b w) -> p b w", b=16)),
                              ["ulast"], ["U"])
                        if sample:
                            V(lambda e, f=f, fc=fc: e.tensor_copy(out=ulast[:, fc, :].rearrange("p (b w) -> p b w", b=16), in_=uview(U, f)[:, :, WW:WW + 2]),
                              ["U"], ["ulast"])
                        elif slot_is_last:
                            V(lambda e, f=f, fc=fc: e.tensor_copy(out=ulast[:, fc, 0:2], in_=U[:, f, NT:NT + 2]), ["U"], ["ulast"])
                        V(lambda e, f=f, fc=fc, cv_=cv_: e.tensor_scalar(out=cv_, in0=uview(U, f)[:, :, 0:WW], scalar1=cw[:, fc, 0:1], scalar2=None, op0=ALU.mult),
                          ["U", "cw", "CGC"], ["CGC"])
                        for tap in (1, 2):
                            V(lambda e, f=f, fc=fc, cv_=cv_, tap=tap: e.scalar_tensor_tensor(out=cv_, in0=uview(U, f)[:, :, tap:tap + WW], scalar=cw[:, fc, tap:tap + 1],
                                                                                             in1=cv_, op0=ALU.mult, op1=ALU.add), ["U", "cw", "CGC"], ["CGC"])
                    elif sname == "bg":
                        V(lambda e, cv_=cv_, bank=bank: e.tensor_tensor(out=cv_, in0=pview(bank), in1=cv_, op=ALU.mult), [PS(bank), "CGC"], ["CGC"])
                    elif sname == "gb":
                        A(lambda e, bank=bank: e.activation(out=SG[:, 0:NT], in_=banks[bank][:, 0:NT], func=AF.Sigmoid), [PS(bank)], ["t2"])
                        V(lambda e, cv_=cv_: e.tensor_tensor(out=cv_, in0=cv_, in1=v3(SG[:, 0:NT]), op=ALU.mult), ["t2", "CGC"], ["CGC"])
                    else:
                        A(lambda e, bank=bank: e.activation(out=SG[:, 0:NT], in_=banks[bank][:, 0:NT], func=AF.Sigmoid), [PS(bank)], ["t2"])
                        V(lambda e, fc=fc: e.tensor_tensor(out=T3[:, 0:NT], in0=SG[:, 0:NT], in1=oT[:, fc, 0:NT], op=ALU.mult), ["t2", "oT"], ["t1"])
                        V(lambda e, fc=fc, cv_=cv_: e.tensor_tensor(out=v3(oT[:, fc, 0:NT]), in0=v3(T3[:, 0:NT]), in1=cv_, op=ALU.add), ["t1", "CGC"], ["oT"])
        if sample or slot_is_last:
            ncol = 32 if sample else 2
            for fc in range(8):
                PE(lambda e, fc=fc: e.transpose(out=banks[6 + fc // 4][0:ncol, (fc % 4) * 128:(fc % 4) * 128 + 128],
                                                in_=ulast[:, fc, 0:ncol], identity=ident32[:, :]), ["ulast", "ident32"], [PS(6 + fc // 4)])
            for half in range(2):
                A(lambda e, half=half: e.copy(out=zs[0:ncol, half * 512:(half + 1) * 512], in_=banks[6 + half][0:ncol, :]), [PS(6 + half)], ["zs"])
            ST((nconv_s if sample else nconv_p)[:, :], zs[0:ncol, :], ["zs"])
        wo = []
        for blk in range(2):
            wk, (wv,) = load_w([(w_out[:, blk * 512:(blk + 1) * 512], 8, 512)])
            wo.append((wv, wk))
        for t in range(ntl):
            for blk in range(2):
                wv, wk = wo[blk]
                for kc in range(8):
                    PE(lambda e, kc=kc, t=t, blk=blk, wv=wv: e.matmul(banks[blk][:, :], lhsT=oT[:, kc, t * 128:(t + 1) * 128], rhs=wv[:, kc, :], start=(kc == 0), stop=(kc == 7)),
                       ["oT", wk], [PS(blk)])
                V(lambda e, t=t, blk=blk: e.tensor_tensor(out=xres[:, t, blk * 512:(blk + 1) * 512], in0=banks[blk][:, :], in1=xres[:, t, blk * 512:(blk + 1) * 512], op=ALU.add),
                  [PS(blk), "xres"], ["xres"])
        norm_T_tiles([(xres[:, t, :], "xres", 1, hT[:, :, t * 128:(t + 1) * 128], "hT") for t in range(ntl)])
        for fh in range(2):
            for (i0, n) in ((0, 2), (2, 2), (4, 2), (6, 2), (8, 2), (10, 1)):
                ci = fh * 11 + i0
                wk, (wg, wu) = load_w([(w_gu[:, ci * 128:(ci + n) * 128], 8, n * 128), (w_gu[:, DFF + ci * 128: DFF + (ci + n) * 128], 8, n * 128)])
                for f in range(n):
                    for kc in range(8):
                        PE(lambda e, kc=kc, f=f, wg=wg: e.matmul(banks[0][:, 0:NT], lhsT=wg[:, kc, f * 128:(f + 1) * 128], rhs=hT[:, kc, 0:NT], start=(kc == 0), stop=(kc == 7)),
                           ["hT", wk], [PS(0)])
                    for kc in range(8):
                        PE(lambda e, kc=kc, f=f, wu=wu: e.matmul(banks[1][:, 0:NT], lhsT=wu[:, kc, f * 128:(f + 1) * 128], rhs=hT[:, kc, 0:NT], start=(kc == 0), stop=(kc == 7)),
                           ["hT", wk], [PS(1)])
                    A(lambda e: e.activation(out=SG[:, 0:NT], in_=banks[0][:, 0:NT], func=AF.Silu), [PS(0)], ["t2"])
                    V(lambda e, i0=i0, f=f: e.tensor_tensor(out=actT[:, i0 + f, 0:NT], in0=banks[1][:, 0:NT], in1=SG[:, 0:NT], op=ALU.mult), [PS(1), "t2"], ["actT"])
            for cb in range(4):
                wk, (wv,) = load_w([(w_dn[fh * 1408:(fh + 1) * 1408, cb * 256:(cb + 1) * 256], 11, 256)])
                for t in range(ntl):
                    bank = t % 2
                    for i in range(11):
                        PE(lambda e, i=i, t=t, bank=bank, wv=wv: e.matmul(banks[bank][:, 0:256], lhsT=actT[:, i, t * 128:(t + 1) * 128], rhs=wv[:, i, :], start=(i == 0), stop=(i == 10)),
                           ["actT", wk], [PS(bank)])
                    V(lambda e, t=t, cb=cb, bank=bank: e.tensor_tensor(out=xres[:, t, cb * 256:(cb + 1) * 256], in0=banks[bank][:, 0:256], in1=xres[:, t, cb * 256:(cb + 1) * 256], op=ALU.add),
                      [PS(bank), "xres"], ["xres"])
        norm_T_tiles([(xres[:, t, :], "xres", 2, hT[:, :, t * 128:(t + 1) * 128], "hT") for t in range(ntl)])
        for t in range(ntl):
            LD(pt_f[:], p_dram[t * 128:(t + 1) * 128, :], ["pt_f"])
            A(lambda e: e.copy(out=pb[:], in_=pt_f[:]), ["pt_f"], ["pb"])
            transpose_to(pT[:, :, t * 128:(t + 1) * 128], "pT", pb, "pb", 2)
        for blk in range(2):
            wk, (wgv, wpv) = load_w([(w_pg[:, blk * 512:(blk + 1) * 512], 8, 512), (w_ple[:, blk * 512:(blk + 1) * 512], 2, 512)])
            for t in range(ntl):
                for kc in range(8):
                    PE(lambda e, kc=kc, t=t, wgv=wgv: e.matmul(banks[0][:, :], lhsT=hT[:, kc, t * 128:(t + 1) * 128], rhs=wgv[:, kc, :], start=(kc == 0), stop=(kc == 7)),
                       ["hT", wk], [PS(0)])
                for kc in range(2):
                    PE(lambda e, kc=kc, t=t, wpv=wpv: e.matmul(banks[1][:, :], lhsT=pT[:, kc, t * 128:(t + 1) * 128], rhs=wpv[:, kc, :], start=(kc == 0), stop=(kc == 1)),
                       ["pT", wk], [PS(1)])
                A(lambda e: e.activation(out=SG[:, :], in_=banks[0][:, :], func=AF.Sigmoid), [PS(0)], ["t2"])
                V(lambda e: e.tensor_tensor(out=T3[:, :], in0=banks[1][:, :], in1=SG[:, :], op=ALU.mult), [PS(1), "t2"], ["t1"])
                V(lambda e, t=t, blk=blk: e.tensor_tensor(out=xres[:, t, blk * 512:(blk + 1) * 512], in0=T3[:, :], in1=xres[:, t, blk * 512:(blk + 1) * 512], op=ALU.add),
                  ["t1", "xres"], ["xres"])
        for t in range(ntl):
            yb, yk = (zs[:, :], "zs") if t % 2 == 0 else (scores[:, 0:1024], "scores")
            norm_tile(xres[:, t, :], "xres", 3, out_hn=yb, okey=yk, sc=t % 2)
            ST(x_dram_out[t * 128:(t + 1) * 128, :], yb, [yk])

    for s in range(4):
        kmax = 4 * max(OWN_A[s], OWN_B[s])
        LD(xt[:], x_halo[s], ["xt"])
        for t in range(4):
            LD(xres[:, t, :], x_own[(s * 4 + t) * 128:(s * 4 + t + 1) * 128, :], ["xres"])
        norm_T_tiles([(xt[:], "xt", 0, hTh[:, :, :], "hTh")] +
                     [(xres[:, t, :], "xres", 0, hT[:, :, t * 128:(t + 1) * 128], "hT") for t in range(4)])
        load_wq()

        def prep(t):
            LD(amask[:], amask_d[s * 4 + t], ["amask"])
            q_proj_tile(t * 128, rope_own[s * 4 + t], qT2[t % 2], "qT%d" % (t % 2))

        prep(0)
        for _ in gen_A(kmax + 1):
            pass
        make_maskT(kmax + 1, neg=True)
        for t in range(4):
            gen, gi = None, 0
            if t + 1 < 4:
                prep(t + 1)
                gen, gi = gen_A(kmax + t + 2), n_items_A(kmax + t + 2)
            stage_C(kmax + t + 1, qT2[t % 2], "qT%d" % (t % 2), gen, gi)
            stage_D(t * 128)
            if t + 1 < 4:
                make_maskT(kmax + t + 2, neg=True)
        dense_tail(512, 4, False, y_p[s * 512:(s + 1) * 512, :], p_own[s * 512:(s + 1) * 512, :], s == 3)

    stc = xres[0:32, 1, :]
    kg = WS[0][:, 0:4096].rearrange("p (a c) -> p a c", a=16)
    vgs = [WS[1][:, 0:4096].rearrange("p (a c) -> p a c", a=16), WS[2][:, 0:4096].rearrange("p (a c) -> p a c", a=16)]
    vgk = ["WS1", "WS2"]
    qiT8 = actT[0:64, 0:2, :].rearrange("p a (h c) -> p (a h) c", c=128)
    kis = [actT[:, 2 + 2 * i:4 + 2 * i, :].rearrange("p a (j c) -> p (a j) c", c=64) for i in range(2)]
    kiTb = kiT[0:64, 0:2048].rearrange("p (j c) -> p j c", c=128)
    STall = xres[:, 2:4, :].rearrange("p a (j c) -> p (a j) c", c=128)
    wiB = xt[:, :]
    rl = zs[:, :]
    ptab_f = t1[:, 0:32]
    wi_scr = nc.dram_tensor("wi_scr", [128, 8], F32).ap()

    LD(xres[:, 0, :], x_smp[:, :], ["xres"])
    norm_tile(xres[:, 0, :], "xres", 0)
    transpose_to(hT[:, :, 0:128], "hT", hn, "hn", 8)
    LD(stc, st_conv[:, :], ["xres1"])
    for fc in range(8):
        PE(lambda e, fc=fc: e.transpose(out=banks[6 + fc // 4][:, (fc % 4) * 32:(fc % 4) * 32 + 32], in_=xres[0:32, 1, fc * 128:(fc + 1) * 128], identity=ident32[0:32, 0:32]),
           ["xres1", "ident32"], [PS(6 + fc // 4)])
    for half in range(2):
        A(lambda e, half=half: e.copy(out=ulast[:, half * 4:(half + 1) * 4, :], in_=banks[6 + half][:, 0:128].rearrange("p (k c) -> p k c", k=4)),
          [PS(6 + half)], ["ulast"])
    load_wkv()
    kv_from_h(hT[:, :, 0:128], "hT", rope_own[16], nk_s[:, :], nv_s[:, :], nki_s[:, :], 16, kTn[:, :, :], "kTn", 2048)
    load_wq()
    LD(amask[:], amask_d[16], ["amask"])
    q_proj_tile(0, rope_own[16], qT2[0], "qT0")
    pb_ = banksb[2]
    for h in range(8):
        PE(lambda e, h=h: e.transpose(out=pb_[0:64, h * 128:(h + 1) * 128], in_=qirb[:, h * 64:(h + 1) * 64], identity=identb[:]), ["qirb", "identb"], [PS(2)])
    A(lambda e: e.copy(out=qiT8, in_=pb_[0:64, :].rearrange("p (h c) -> p h c", h=8)), [PS(2)], ["actT"])
    B.dma("sp", wi_scr, wi[:, :], ["wi"], ["wi_scr"])
    B.dma("sp", wiB, wi_scr.rearrange("a h -> (a h)").partition_broadcast(128), ["wi_scr"], ["xt"])
    wiBv = wiB.rearrange("p (b t h) -> p b h t", b=16, t=8)
    LD(ptab_i[:], ptab[:, :], ["ptab_i"])
    V(lambda e: e.tensor_copy(out=ptab_f, in_=ptab_i[:]), ["ptab_i"], ["t1"])
    V(lambda e: e.tensor_scalar(out=ptab_f, in0=ptab_f, scalar1=16.0, scalar2=piota[:, 0:1], op0=ALU.mult, op1=ALU.add), ["t1", "piota"], ["t1"])
    V(lambda e: e.tensor_copy(out=gidx[:], in_=ptab_f), ["t1"], ["gidx"])

    def gather_ki(b):
        for half in range(2):
            B.dma("pool", kis[b % 2][:, half * 8:(half + 1) * 8, :].rearrange("p a c -> p (a c)"), cki, ["gidx"], ["kis%d" % (b % 2)], indirect=gidx[:, b * 2 + half:b * 2 + half + 1])

    gather_ki(0)
    for b in range(16):
        if b + 1 < 16:
            gather_ki(b + 1)
        kb, kk = kis[b % 2], "kis%d" % (b % 2)
        for g0 in (0, 8):
            for k in range(8):
                PE(lambda e, g0=g0, k=k, kb=kb: e.transpose(out=pb_[0:64, k * 128:(k + 1) * 128], in_=kb[:, g0 + k, :], identity=identb[:]), [kk, "identb"], [PS(2)])
            A(lambda e, g0=g0: e.copy(out=kiTb[:, g0:g0 + 8, :], in_=pb_[0:64, :].rearrange("p (k c) -> p k c", k=8)), [PS(2)], ["kiT"])
        for j in range(16):
            PE(lambda e, j=j, b=b: e.matmul(banks[j // 8][:, (j % 8) * 64:(j % 8) * 64 + 64].rearrange("p (h t) -> p h t", h=8), lhsT=kiTb[:, j, :],
                                            rhs=qiT8[:, :, b * 8:(b + 1) * 8], start=True, stop=True), ["kiT", "actT"], [PS(j // 8)])
        for bk in range(2):
            A(lambda e, bk=bk: e.activation(out=rl[:, bk * 512:(bk + 1) * 512], in_=banks[bk][:, :], func=AF.Relu), [PS(bk)], ["zs"])
        V(lambda e, b=b: e.tensor_tensor(out=rl.rearrange("p (j h t) -> p j h t", j=16, h=8), in0=rl.rearrange("p (j h t) -> p j h t", j=16, h=8),
                                         in1=wiBv[:, b, :, :].unsqueeze(1).to_broadcast([128, 16, 8, 8]), op=ALU.mult), ["zs", "xt"], ["zs"])
        V(lambda e, b=b: e.tensor_reduce(out=STall[:, :, b * 8:(b + 1) * 8], in_=rl.rearrange("p (j h t) -> p j t h", j=16, h=8), axis=AX.X, op=ALU.add),
          ["zs"], ["xres23"])
    for j0 in range(0, 16, 4):
        for k in range(4):
            PE(lambda e, j0=j0, k=k: e.transpose(out=banks[j0 // 4 % 2][:, k * 128:(k + 1) * 128], in_=STall[:, j0 + k, :], identity=ident32[:, :]),
               ["xres23", "ident32"], [PS(j0 // 4 % 2)])
        A(lambda e, j0=j0: e.copy(out=scores[:, j0 * 128:(j0 + 4) * 128], in_=banks[j0 // 4 % 2][:, :]), [PS(j0 // 4 % 2)], ["scores"])
    V(lambda e: e.memset(sm[:, 12:13], 0.0), [], ["kis0", "kis1", "actT", "sm12"])
    for h in range(8):
        bank = h % 2
        r0 = (h % 2) * 64
        PE(lambda e, h=h, r0=r0, bank=bank: e.matmul(banks[bank][:, 0:128], lhsT=qiT[r0:r0 + 64, h // 2, :], rhs=kiT[r0:r0 + 64, 2048:2176], start=True, stop=True),
           ["qiT", "kiT"], [PS(bank)])
        indexer_chunk_epilogue(h, 2048, 128, bank)
    score_bounds(2176)
    V(lambda e: e.tensor_tensor(out=scores[:, 2048:2176], in0=scores[:, 2048:2176], in1=amask[:, 0:128], op=ALU.add), ["scores", "amask"], ["scores"])
    topk_mask(2176, 17)

    def gather_kv(b):
        for half in range(2):
            B.dma("pool", kg[:, half * 8:(half + 1) * 8, :].rearrange("p a c -> p (a c)"), ck, ["gidx"], ["WS0"], indirect=gidx[:, b * 2 + half:b * 2 + half + 1])
        for half in range(2):
            B.dma("pool", vgs[b % 2][:, half * 8:(half + 1) * 8, :].rearrange("p a c -> p (a c)"), cv, ["gidx"], [vgk[b % 2]], indirect=gidx[:, b * 2 + half:b * 2 + half + 1])

    gather_kv(0)
    for b in range(16):
        vg, vk = vgs[b % 2], vgk[b % 2]
        for g0 in range(0, 32, 8):
            for k in range(8):
                pg, kv = (g0 + k) // 2, (g0 + k) % 2
                PE(lambda e, k=k, pg=pg, kv=kv: e.transpose(out=pb_[:, k * 128:(k + 1) * 128], in_=kg[:, pg, kv * 128:(kv + 1) * 128], identity=identb[:]),
                   ["WS0", "identb"], [PS(2)])
            A(lambda e, g0=g0: e.copy(out=KT[:, :, (g0 // 2) * 128:(g0 // 2) * 128 + 512].rearrange("p k (g c) -> p g k c", g=4),
                                      in_=pb_[:, :].rearrange("p (g k c) -> p g k c", g=4, k=2)), [PS(2)], ["KT"])
        if b + 1 < 16:
            gather_kv(b + 1)
        for j in range(17):
            bank = 3 + j // 8
            for kv in range(2):
                c0 = (j % 8) * 64 + kv * 32
                lhs = KT[:, kv, j * 128:(j + 1) * 128] if j < 16 else kTn[:, kv, :]
                PE(lambda e, kv=kv, bank=bank, c0=c0, b=b, lhs=lhs: e.matmul(banks[bank][:, c0:c0 + 32].rearrange("p (g t) -> p g t", g=4), lhsT=lhs,
                                                                           rhs=qT[:, kv * 4:(kv + 1) * 4, b * 8:(b + 1) * 8], start=True, stop=True),
                   ["KT", "kTn", "qT0"], [PS(bank)])
        for bk in range(3):
            n = 8 if bk < 2 else 1
            A(lambda e, bk=bk, n=n: e.activation(out=eS[:, bk * 8:bk * 8 + n, :], in_=banks[3 + bk][:, 0:n * 64].rearrange("p (j c) -> p j c", j=n), func=AF.Exp, scale=128.0 ** -0.5),
              [PS(3 + bk)], ["eS"])
        V(lambda e, b=b: e.tensor_tensor(out=eS[:, :, :].rearrange("p j (h t) -> p j h t", h=8), in0=eS[:, :, :].rearrange("p j (h t) -> p j h t", h=8),
                                         in1=maskT[:, 0:17, b * 8:(b + 1) * 8].unsqueeze(2).to_broadcast([128, 17, 8, 8]), op=ALU.mult), ["eS", "maskT"], ["eS"])
        for kv in range(2):
            for j in range(17):
                vl = vg[:, j, kv * 128:(kv + 1) * 128] if j < 16 else VA[:, 16, kv, 0:128]
                PE(lambda e, j=j, kv=kv, vl=vl: e.matmul(banks[6][:, kv * 32:(kv + 1) * 32], lhsT=vl, rhs=eS[:, j, kv * 32:(kv + 1) * 32], start=(j == 0), stop=(j == 16)),
                   ["VA", vk, "eS"], [PS(6)])
        for kv in range(2):
            for j in range(17):
                PE(lambda e, j=j, kv=kv: e.matmul(banks[7][:, kv * 32:(kv + 1) * 32], lhsT=onesb[:, :], rhs=eS[:, j, kv * 32:(kv + 1) * 32], start=(j == 0), stop=(j == 16)),
                   ["onesb", "eS"], [PS(7)])
        V(lambda e: e.reciprocal(out=rden[:, :], in_=banks[7][:, 0:64]), [PS(7)], ["rden"])
        V(lambda e, b=b: e.tensor_tensor(out=oT[:, :, b * 8:(b + 1) * 8], in0=banks[6][:, 0:64].rearrange("p (h t) -> p h t", h=8),
                                         in1=rden[:, :].rearrange("p (h t) -> p h t", h=8), op=ALU.mult), [PS(6), "rden"], ["oT"])
    if debug:
        dbg1 = dout("dbg1", [128, D])
        dbg2 = dout("dbg2", [128, 2176])
        dbg3 = dout("dbg3", [128, 16])
        V(lambda e: e.tensor_copy(out=zs[:, :].rearrange("p (h c) -> p h c", h=8), in_=oT[:, :, 0:128]), ["oT"], ["zs"])
        ST(dbg1, zs[:, :], ["zs"])
        ST(dbg2, scores[:, 0:2176], ["scores"])
        ST(dbg3, sm[:, :], ["lo", "hi", "cnt"])
        dbgK = dout("dbgK", [128, 2, 2048], BF16)
        dbgV = dout("dbgV", [128, 17, 2, 129], BF16)
        dbgE = dout("dbgE", [128, 17, 64], BF16)
        dbgM = dout("dbgM", [128, 17, 128], BF16)
        dbgQ = dout("dbgQ", [128, 8, 128], BF16)
        ST(dbgK, KT[:, :, 0:2048], ["KT"])
        ST(dbgV, VA[:, 0:17, :, :], ["VA"])
        ST(dbgE, eS[:, :, :], ["eS"])
        ST(dbgM, maskT[:, 0:17, :], ["maskT"])
        ST(dbgQ, qT[:, :, :], ["qT0"])
    dense_tail(128, 1, True, y_s, p_smp, False)

    B.finish()
    return nc


_CACHE = {}


def _rope_tab(pos):
    pos = np.asarray(pos, np.float32)
    out = np.zeros((pos.shape[0], 192), np.float32)
    for (half, o) in ((64, 0), (32, 128)):
        freqs = (np.float32(10000.0) ** (-np.arange(half, dtype=np.float32) / np.float32(half))).astype(np.float32)
        ang = pos[:, None] * freqs[None, :]
        out[:, o:o + half] = np.cos(ang)
        out[:, o + half:o + 2 * half] = np.sin(ang)
    return out


def kernel(x_prompt, x_sample, cache_k, cache_v, cache_kidx, state_conv, page_table,
           p_prompt, p_sample, norm_mix, w_in, conv_w, w_out, norm_ffn, w_gate_up, w_down,
           norm_ple, w_ple, w_ple_gate, norm_final):
    f = np.float32
    if "nc" not in _CACHE:
        _CACHE["nc"] = build_program(debug=bool(_CACHE.get("debug")))
    nc = _CACHE["nc"]
    ck = np.ascontiguousarray(np.asarray(cache_k, f).reshape(NPHYS * 16, 2048))
    cv = np.ascontiguousarray(np.asarray(cache_v, f).reshape(NPHYS * 16, 2048))
    cki = np.ascontiguousarray(np.asarray(cache_kidx, f).reshape(NPHYS * 16, 512))
    gains = np.stack([np.broadcast_to(np.asarray(g, f).reshape(1, D), (128, D)) for g in (norm_mix, norm_ffn, norm_ple, norm_final)]).astype(f)
    convw = np.ascontiguousarray(np.asarray(conv_w, f).reshape(3, 8, 128).transpose(2, 1, 0))
    ident = np.eye(128, dtype=f)
    rope_seq = np.stack([_rope_tab(np.arange(j * 128, (j + 1) * 128)) for j in range(NTILE)])
    piota = (np.arange(128) % 16).astype(f).reshape(128, 1)
    shared = dict(ck=ck, cv=cv, cki=cki, w_in=np.asarray(w_in, f)[0], w_out=np.asarray(w_out, f)[0], w_gu=np.asarray(w_gate_up, f)[0],
                  w_dn=np.asarray(w_down, f)[0], w_pg=np.asarray(w_ple_gate, f)[0], w_ple=np.asarray(w_ple, f)[0], gains=gains, convw=convw,
                  ident=ident, rope_seq=rope_seq, piota=piota)
    in_maps = []
    qpos_l = np.arange(128)
    for c in range(8):
        bseq, par = c // 2, c % 2
        own = OWN_A if par == 0 else OWN_B
        xs = np.asarray(x_prompt, f)[bseq]
        ps = np.asarray(p_prompt, f)[0, bseq]
        x_own = np.concatenate([xs[s * 512:(s + 1) * 512] for s in own])
        p_own = np.concatenate([ps[s * 512:(s + 1) * 512] for s in own])
        x_halo = np.zeros((4, 128, D), f)
        for i, s in enumerate(own):
            if s > 0:
                x_halo[i, 0:2] = xs[s * 512 - 2:s * 512]
        rope_own = np.zeros((17, 128, 192), f)
        amask = np.zeros((17, 128, 640), f)
        for i, s in enumerate(own):
            kmax = 4 * max(OWN_A[i], OWN_B[i])
            for t in range(4):
                qpos = (s * 4 + t) * 128 + qpos_l
                rope_own[i * 4 + t] = _rope_tab(qpos)
                nkt = kmax + t + 1
                kpos = np.arange((nkt - 5) * 128, nkt * 128)
                amask[i * 4 + t] = np.where(kpos[None, :] <= qpos[:, None], 0.0, -BIG)
        rope_own[16] = _rope_tab(2048 + (qpos_l % 8))
        bq, tq = qpos_l // 8, qpos_l % 8
        amask[16, :, 0:128] = np.where((bq[None, :] == bq[:, None]) & (tq[None, :] <= tq[:, None]), 0.0, -BIG)
        ptc = np.asarray(page_table, np.int32)[c * 16:(c + 1) * 16]
        pl = np.arange(128) // 16
        pt = np.stack([ptc[bb, hf * 8 + pl] for bb in range(16) for hf in range(2)], axis=1).astype(np.int32)
        m = dict(shared)
        m.update(x_seq=xs, x_own=x_own, x_halo=x_halo, p_own=p_own,
                 x_smp=np.asarray(x_sample, f)[c * 16:(c + 1) * 16].reshape(128, D),
                 p_smp=np.asarray(p_sample, f)[0, c * 16:(c + 1) * 16].reshape(128, 256),
                 st_conv=np.asarray(state_conv, f)[0, c * 16:(c + 1) * 16].reshape(32, D),
                 ptab=np.ascontiguousarray(pt),
                 rope_own=rope_own, amask=amask)
        in_maps.append({k: np.ascontiguousarray(v) for k, v in m.items()})
    res = run_bass_kernel_spmd(nc, in_maps, core_ids=list(range(8)))
    R = res.results
    _CACHE['R'] = R
    y_prompt = np.zeros((4, SEQ, D), f)
    for c in range(8):
        own = OWN_A if c % 2 == 0 else OWN_B
        for i, s in enumerate(own):
            y_prompt[c // 2, s * 512:(s + 1) * 512] = R[c]["y_p"][i * 512:(i + 1) * 512]
    y_sample = np.concatenate([R[c]["y_s"].reshape(16, 8, D) for c in range(8)], 0)
    nk_p = np.stack([R[2 * b]["nk_p"].reshape(SEQ, 2, 128) for b in range(4)])[None]
    nv_p = np.stack([R[2 * b]["nv_p"].reshape(SEQ, 2, 128) for b in range(4)])[None]
    nki_p = np.stack([R[2 * b]["nki_p"] for b in range(4)])[None]
    nconv_p = np.stack([R[2 * b]["nconv_p"] for b in range(4)])[None]
    nk_s = np.concatenate([R[c]["nk_s"].reshape(16, 8, 2, 128) for c in range(8)], 0)[None]
    nv_s = np.concatenate([R[c]["nv_s"].reshape(16, 8, 2, 128) for c in range(8)], 0)[None]
    nki_s = np.concatenate([R[c]["nki_s"].reshape(16, 8, 64) for c in range(8)], 0)[None]
    nconv_s = np.concatenate([R[c]["nconv_s"].reshape(16, 2, D) for c in range(8)], 0)[None]
    outs = (y_prompt, y_sample, nk_p, nv_p, nki_p, nconv_p, nk_s, nv_s, nki_s, nconv_s)
    return tuple(np.ascontiguousarray(o, dtype=f) for o in outs)
```

```python
import numpy as np
import concourse.bass as bass
import concourse.mybir as mybir
from concourse.bass_utils import run_bass_kernel_spmd

F32 = mybir.dt.float32
BF16 = mybir.dt.bfloat16
I32 = mybir.dt.int32
AF = mybir.ActivationFunctionType
ALU = mybir.AluOpType
AX = mybir.AxisListType

D = 1024
SEQ = 4096
NTILE = 32
DFF = 2816
NFF = 22
DIN = 7240
NPHYS = 2560
BIG = 1.0e30
EPS = 1e-6
NSEL = 256
NBIS = 12
OWN_A = [0, 3, 4, 7]
OWN_B = [1, 2, 5, 6]
O_Q, O_K, O_V, O_QI, O_KI, O_WI, O_BG, O_CG, O_XC, O_GA, O_GB = 0, 1024, 1280, 1536, 2048, 2112, 2120, 3144, 4168, 5192, 6216


class Builder:
    def __init__(self, nc):
        self.nc = nc
        self.eng = {}
        for name, h in (("pe", nc.tensor), ("act", nc.scalar), ("dve", nc.vector), ("pool", nc.gpsimd), ("sp", nc.sync)):
            self.eng[name] = dict(h=h, sem=nc.semaphore("sem_" + name).__enter__(), cnt=0, seen={}, name=name)
        self.rings = {}
        for q in ("sp", "pool"):
            self.rings[q] = dict(sems=[nc.semaphore("dq_%s_%d" % (q, i)).__enter__() for i in range(8)], n=0)
        self.writer = {}
        self.readers = {}
        self.out_events = []

    def _need(self, e, ev, same_ok):
        if ev is None:
            return
        sem, val, owner, semname = ev
        if owner == e["name"] and same_ok:
            return
        if e["seen"].get(semname, 0) >= val:
            return
        e["h"].wait_ge(sem, val)
        e["seen"][semname] = val

    def _deps(self, e, reads, writes, pe_acc=False):
        for r in reads:
            self._need(e, self.writer.get(r), False)
        for w in writes:
            self._need(e, self.writer.get(w), True)
            for ev in self.readers.get(w, {}).values():
                self._need(e, ev, True)

    def _commit(self, ev, reads, writes):
        for r in reads:
            self.readers.setdefault(r, {})[ev[2] + ev[3]] = ev
        for w in writes:
            self.writer[w] = ev
            self.readers[w] = {}

    def op(self, engname, fn, reads, writes):
        e = self.eng[engname]
        self._deps(e, reads, writes)
        inst = fn(e["h"])
        e["cnt"] += 1
        inst.then_inc(e["sem"], 1)
        ev = (e["sem"], e["cnt"], engname, "E" + engname)
        self._commit(ev, reads, writes)
        return ev

    def dma(self, q, out, in_, reads, writes, indirect=None, is_out=False):
        e = self.eng[q]
        ring = self.rings[q]
        n = ring["n"]
        slot = n % 8
        sem = ring["sems"][slot]
        semname = "R%s%d" % (q, slot)
        self._deps(e, reads, writes)
        if n >= 8:
            need = 16 * (n // 8)
            if e["seen"].get(semname, 0) < need:
                e["h"].wait_ge(sem, need)
                e["seen"][semname] = need
        if indirect is not None:
            inst = e["h"].indirect_dma_start(out=out, out_offset=None, in_=in_,
                                             in_offset=bass.IndirectOffsetOnAxis(ap=indirect, axis=0))
        else:
            inst = e["h"].dma_start(out=out, in_=in_)
        inst.then_inc(sem, 16)
        ring["n"] = n + 1
        ev = (sem, 16 * (n // 8 + 1), "dma" + q, semname)
        self._commit(ev, reads, writes)
        if is_out:
            self.out_events.append(ev)
        return ev

    def finish(self):
        e = self.eng["sp"]
        for ev in self.out_events:
            self._need(e, ev, False)
        for name in ("pe", "act", "dve", "pool"):
            o = self.eng[name]
            if o["cnt"] > 0:
                self._need(e, (o["sem"], o["cnt"], name, "E" + name), False)


def build_program(debug=False):
    nc = bass.Bass("TRN2", target_bir_lowering=False)
    B = Builder(nc)

    def din(name, shape, dt=F32):
        return nc.dram_tensor(name, shape, dt, kind="ExternalInput").ap()

    def dout(name, shape, dt=F32):
        return nc.dram_tensor(name, shape, dt, kind="ExternalOutput").ap()

    x_seq = din("x_seq", [SEQ, D])
    x_own = din("x_own", [2048, D])
    x_halo = din("x_halo", [4, 128, D])
    p_own = din("p_own", [2048, 256])
    x_smp = din("x_smp", [128, D])
    p_smp = din("p_smp", [128, 256])
    st_conv = din("st_conv", [32, D])
    ptab = din("ptab", [128, 32], I32)
    ck = din("ck", [NPHYS * 16, 2048])
    cv = din("cv", [NPHYS * 16, 2048])
    cki = din("cki", [NPHYS * 16, 512])
    w_in = din("w_in", [D, DIN])
    w_out = din("w_out", [D, D])
    w_gu = din("w_gu", [D, 2 * DFF])
    w_dn = din("w_dn", [DFF, D])
    w_pg = din("w_pg", [D, D])
    w_ple = din("w_ple", [256, D])
    gains = din("gains", [4, 128, D])
    convw = din("convw", [128, 8, 3])
    ident_d = din("ident", [128, 128])
    rope_seq = din("rope_seq", [NTILE, 128, 192])
    rope_own = din("rope_own", [17, 128, 192])
    amask_d = din("amask", [17, 128, 640])
    piota_d = din("piota", [128, 1])

    y_p = dout("y_p", [2048, D])
    y_s = dout("y_s", [128, D])
    nk_p = dout("nk_p", [SEQ, 256])
    nv_p = dout("nv_p", [SEQ, 256])
    nki_p = dout("nki_p", [SEQ, 64])
    nconv_p = dout("nconv_p", [2, D])
    nk_s = dout("nk_s", [128, 256])
    nv_s = dout("nv_s", [128, 256])
    nki_s = dout("nki_s", [128, 64])
    nconv_s = dout("nconv_s", [32, D])

    def sb(name, shape, dt):
        return nc.sbuf_tensor(name, shape, dt).__enter__()

    banks = [nc.psum_tensor("bank%d" % i, [128, 512], F32).__enter__() for i in range(8)]
    banksb = [b.bitcast(BF16) for b in banks]

    def PS(i):
        return "ps%d" % i

    KT = sb("KT", [128, 2, SEQ], BF16)
    VA = sb("VA", [128, 32, 2, 129], BF16)
    kiT = sb("kiT", [128, SEQ], BF16)
    kTn = sb("kTn", [128, 2, 128], BF16)
    identb = sb("identb", [128, 128], BF16)
    ident32 = sb("ident32_sb", [128, 128], F32)
    G = sb("G", [128, D], F32)
    cw = sb("cw", [128, 8, 3], F32)
    epsT = sb("epsT", [128, 1], F32)
    halfT = sb("halfT", [128, 1], F32)
    onesb = sb("onesb", [128, 128], BF16)
    xres = sb("xres", [128, 4, D], F32)
    hn = sb("hn", [128, D], BF16)
    hT = sb("hT", [128, 8, 512], BF16)
    hTh = sb("hTh", [128, 8, 128], BF16)
    sm = sb("sm", [128, 16], F32)
    rope_t = sb("rope_t", [128, 192], F32)
    zs = sb("zs", [128, D], F32)
    t1 = sb("t1", [128, 512], F32)
    t2 = sb("t2", [128, 512], F32)
    T3 = t1
    SG = t2
    kout = sb("kout", [128, 256], F32)
    kiout = sb("kiout", [128, 64], F32)
    krb = sb("krb", [128, 256], BF16)
    kib2 = sb("kib2", [128, 128], BF16)
    qrb = sb("qrb", [128, D], BF16)
    qirb = sb("qirb", [128, 512], BF16)
    qT2 = [sb("qT%d" % i, [128, 8, 128], BF16) for i in range(2)]
    qT = qT2[0]
    qiT = sb("qiT", [128, 4, 128], BF16)
    wi = sb("wi", [128, 8], F32)
    absw = sb("absw", [128, 8], F32)
    sgnw = sb("sgnw", [128, 8], F32)
    scores = sb("scores", [128, SEQ], F32)
    maskT = sb("maskT", [128, 32, 128], BF16)
    maskc = sb("maskc", [128, 1024], BF16)
    junk = sb("junk", [128, SEQ], mybir.dt.uint8)
    on_bt = sb("on_bt", [128, 8, 128], BF16)
    den8 = sb("den8", [128, 16], F32)
    pw = sb("pw", [128, 32], F32)
    esum = sb("esum", [128, 64], F32)
    ones32 = sb("ones32", [128, 128], F32)
    hk = sb("hk", [128, 64], F32)
    rbuf = [sb("rbuf%d" % i, [128, 512], F32) for i in range(2)]
    ebuf = [sb("ebuf%d" % i, [128, 512], BF16) for i in range(3)]
    amask = sb("amask_t", [128, 640], F32)
    oT = sb("oT", [128, 8, 512], BF16)
    WS = [sb("WS%d" % i, [128, 5120], BF16) for i in range(3)]
    CGC = sb("CGC", [128, 2, 514], F32)
    U = sb("U", [128, 2, 514], F32)
    ulast = sb("ulast", [128, 8, 32], F32)
    actT = sb("actT", [128, 11, 512], BF16)
    pt_f = sb("pt_f", [128, 256], F32)
    pb = sb("pb", [128, 256], BF16)
    pT = sb("pT", [128, 2, 512], BF16)
    ptab_i = sb("ptab_i", [128, 32], I32)
    piota = sb("piota_sb", [128, 1], F32)
    gidx = sb("gidx", [128, 32], I32)
    eS = sb("eS", [128, 17, 64], BF16)
    rden = sb("rden", [128, 64], F32)

    ws_n = [0]
    xt = sb("xt", [128, D], F32)
    yt = zs
    gcur = [-1]

    def V(fn, reads, writes):
        return B.op("dve", fn, reads, writes)

    def A(fn, reads, writes):
        return B.op("act", fn, reads, writes)

    def PE(fn, reads, writes):
        return B.op("pe", fn, reads, writes)

    def LD(out, in_, writes, reads=()):
        return B.dma("sp", out, in_, list(reads), list(writes))

    def LDC(out, in_, writes, reads=()):
        return B.dma("pool", out, in_, list(reads), list(writes))

    def ST(out, in_, reads):
        return B.dma("sp", out, in_, list(reads), [], is_out=True)

    LDC(identb[:], ident_d, ["identb"])
    LD(ident32[:], ident_d, ["ident32"])
    LD(cw[:], convw, ["cw"])
    LD(piota[:], piota_d, ["piota"])
    V(lambda e: e.memset(epsT[:], EPS), [], ["epsT"])
    V(lambda e: e.memset(halfT[:], 0.5), [], ["halfT"])
    V(lambda e: e.memset(sm[:, 11:12], -30000.0), [], ["negb"])
    V(lambda e: e.memset(ones32[:], 1.0), [], ["ones32"])
    V(lambda e: e.memset(onesb[:], 1.0), [], ["onesb"])
    V(lambda e: e.memset(VA[:, :, :, 128:129], 1.0), [], ["VA"])
    for k in range(NBIS + 1):
        V(lambda e, k=k: e.memset(pw[:, k:k + 1], 0.5 ** (k + 1)), [], ["pw"])

    def norm_tile(x_ap, xkey, gi, out_hn=hn, okey="hn", sc=0):
        if gcur[0] != gi:
            LD(G[:], gains[gi], ["G"])
            gcur[0] = gi
        c0 = 13 if sc else 0
        k0, k1, k2 = "sm%d" % c0, "sm%d" % (c0 + 1), "sm%d" % (c0 + 2)
        A(lambda e: e.activation(out=out_hn[:], in_=x_ap, func=AF.Square, accum_out=sm[:, c0:c0 + 1]), [xkey], [okey, k0])
        A(lambda e: e.activation(out=sm[:, c0 + 1:c0 + 2], in_=sm[:, c0:c0 + 1], func=AF.Sqrt, scale=1.0 / D, bias=epsT[:, 0:1]), [k0, "epsT"], [k1])
        V(lambda e: e.reciprocal(out=sm[:, c0 + 2:c0 + 3], in_=sm[:, c0 + 1:c0 + 2]), [k1], [k2])
        V(lambda e: e.scalar_tensor_tensor(out=out_hn[:], in0=x_ap, scalar=sm[:, c0 + 2:c0 + 3], in1=G[:], op0=ALU.mult, op1=ALU.mult),
          [xkey, k2, "G", okey], [okey])

    def transpose_to(dst_ap, dkey, src, skey, nblk, bank=2):
        pb_ = banksb[bank]
        for k in range(nblk):
            PE(lambda e, k=k: e.transpose(out=pb_[:, k * 128:(k + 1) * 128], in_=src[:, k * 128:(k + 1) * 128], identity=identb[:]),
               [skey, "identb"], [PS(bank)])
        A(lambda e: e.copy(out=dst_ap, in_=pb_[:, 0:nblk * 128].rearrange("p (k c) -> p k c", k=nblk)), [PS(bank)], [dkey])

    def norm_T_tiles(items):
        bufs = [(hn, "hn"), (qrb, "qrb")]

        def n_(i):
            x_ap, xkey, gi, _, _ = items[i]
            b_, k_ = bufs[i % 2]
            norm_tile(x_ap, xkey, gi, out_hn=b_, okey=k_, sc=i % 2)

        def t_(i):
            _, _, _, dst, dkey = items[i]
            b_, k_ = bufs[i % 2]
            transpose_to(dst, dkey, b_, k_, 8, bank=(2 if i % 2 == 0 else 7))

        n_(0)
        for i in range(len(items)):
            if i + 1 < len(items):
                n_(i + 1)
            t_(i)

    def rope(src, skey, H, Dh, cos, sin, dst, dkey, eng="dve"):
        half = Dh // 2
        s3 = src.rearrange("p (h d) -> p h d", h=H)
        d3 = dst.rearrange("p (h d) -> p h d", h=H)
        x1, x2 = s3[:, :, 0:half], s3[:, :, half:Dh]
        cb = cos.unsqueeze(1).to_broadcast([128, H, half])
        sbb = sin.unsqueeze(1).to_broadcast([128, H, half])
        a1 = t1[:, 0:H * half].rearrange("p (h d) -> p h d", h=H)
        a2 = t2[:, 0:H * half].rearrange("p (h d) -> p h d", h=H)

        def X(fn, r, w):
            return B.op(eng, fn, r, w)
        X(lambda e: e.tensor_tensor(out=a1, in0=x1, in1=cb, op=ALU.mult), [skey, "rope_t"], ["t1"])
        X(lambda e: e.tensor_tensor(out=a2, in0=x2, in1=sbb, op=ALU.mult), [skey, "rope_t"], ["t2"])
        X(lambda e: e.tensor_tensor(out=d3[:, :, 0:half], in0=a1, in1=a2, op=ALU.subtract), ["t1", "t2"], [dkey])
        X(lambda e: e.tensor_tensor(out=a1, in0=x2, in1=cb, op=ALU.mult), [skey, "rope_t"], ["t1"])
        X(lambda e: e.tensor_tensor(out=a2, in0=x1, in1=sbb, op=ALU.mult), [skey, "rope_t"], ["t2"])
        X(lambda e: e.tensor_tensor(out=d3[:, :, half:Dh], in0=a1, in1=a2, op=ALU.add), ["t1", "t2"], [dkey])

    def load_w(src_list):
        i = ws_n[0] % 3
        ws_n[0] += 1
        off = 0
        views = []
        for (ap, nk, ncol) in src_list:
            v = WS[i][:, off:off + nk * ncol].rearrange("p (k n) -> p k n", k=nk)
            LDC(v, ap.rearrange("(k p) n -> p k n", p=128), ["WS%d" % i])
            views.append(v)
            off += nk * ncol
        assert off <= 5120
        return "WS%d" % i, views

    wkv = {}

    def load_wkv():
        k, (a_, b_) = load_w([(w_in[:, O_K:O_K + 512], 8, 512), (w_in[:, O_KI:O_KI + 64], 8, 64)])
        wkv["key"], wkv["kv"], wkv["ki"] = k, a_, b_

    BS = [dict(zs=zs[:, :], zk="zs", kout=kout[:, :], kok="kout", kiout=kiout[:, :], kik="kiout", krb=krb[:, :], krk="krb", kib2=kib2[:, :], k2k="kib2",
               rt=rope_t[:, :], rtk="rope_t", b0=0, b1=1, b2=2),
          dict(zs=scores[:, 0:1024], zk="scores", kout=scores[:, 1024:1280], kok="scoresB", kiout=scores[:, 1280:1344], kik="scoresC",
               krb=maskc[:, 0:256], krk="maskc", kib2=maskc[:, 256:384], k2k="maskcB", rt=amask[:, 0:192], rtk="amask", b0=3, b1=4, b2=5)]

    def rope2(src, skey, H, Dh, cos, sin, rtk, dst, dkey):
        half = Dh // 2
        s3 = src.rearrange("p (h d) -> p h d", h=H)
        d3 = dst.rearrange("p (h d) -> p h d", h=H)
        x1, x2 = s3[:, :, 0:half], s3[:, :, half:Dh]
        cb = cos.unsqueeze(1).to_broadcast([128, H, half])
        sbb = sin.unsqueeze(1).to_broadcast([128, H, half])
        a1 = t1[:, 0:H * half].rearrange("p (h d) -> p h d", h=H)
        a2 = t2[:, 0:H * half].rearrange("p (h d) -> p h d", h=H)
        V(lambda e: e.tensor_tensor(out=a1, in0=x1, in1=cb, op=ALU.mult), [skey, rtk], ["t1"])
        V(lambda e: e.tensor_tensor(out=a2, in0=x2, in1=sbb, op=ALU.mult), [skey, rtk], ["t2"])
        V(lambda e: e.tensor_tensor(out=d3[:, :, 0:half], in0=a1, in1=a2, op=ALU.subtract), ["t1", "t2"], [dkey])
        V(lambda e: e.tensor_tensor(out=a1, in0=x2, in1=cb, op=ALU.mult), [skey, rtk], ["t1"])
        V(lambda e: e.tensor_tensor(out=a2, in0=x1, in1=sbb, op=ALU.mult), [skey, rtk], ["t2"])
        V(lambda e: e.tensor_tensor(out=d3[:, :, half:Dh], in0=a1, in1=a2, op=ALU.add), ["t1", "t2"], [dkey])

    def kv_from_h_gen(hT_ap, hkey, rope_src_ap, k_dst, v_dst, ki_dst, va_tile, kt_dst, kt_key, kit_cols, bs=0):
        wkv_v, wki_v, wkv_key = wkv["kv"], wkv["ki"], wkv["key"]
        S_ = BS[bs]
        z, zk, ko, kok, kio, kik = S_["zs"], S_["zk"], S_["kout"], S_["kok"], S_["kiout"], S_["kik"]
        kr, krk, k2, k2k, rt, rtk = S_["krb"], S_["krk"], S_["kib2"], S_["k2k"], S_["rt"], S_["rtk"]
        b0, b1, b2 = S_["b0"], S_["b1"], S_["b2"]
        LD(rt, rope_src_ap, [rtk])
        for kc in range(8):
            PE(lambda e, kc=kc: e.matmul(banks[b0][:, :], lhsT=hT_ap[:, kc, :], rhs=wkv_v[:, kc, :], start=(kc == 0), stop=(kc == 7)),
               [hkey, wkv_key], [PS(b0)])
        for kc in range(8):
            PE(lambda e, kc=kc: e.matmul(banks[b1][:, 0:64], lhsT=hT_ap[:, kc, :], rhs=wki_v[:, kc, :], start=(kc == 0), stop=(kc == 7)),
               [hkey, wkv_key], [PS(b1)])
        A(lambda e: e.copy(out=z[:, 0:512], in_=banks[b0][:, :]), [PS(b0)], [zk])
        A(lambda e: e.copy(out=z[:, 512:576], in_=banks[b1][:, 0:64]), [PS(b1)], [zk])
        yield
        rope2(z[:, 0:256], zk, 2, 128, rt[:, 0:64], rt[:, 64:128], rtk, ko, kok)
        rope2(z[:, 512:576], zk, 1, 64, rt[:, 128:160], rt[:, 160:192], rtk, kio, kik)
        ST(k_dst, ko, [kok])
        ST(v_dst, z[:, 256:512], [zk])
        ST(ki_dst, kio, [kik])
        A(lambda e: e.copy(out=kr, in_=ko), [kok], [krk])
        A(lambda e: e.copy(out=k2[:, 0:64], in_=kio), [kik], [k2k])
        A(lambda e: e.copy(out=k2[:, 64:128], in_=kio), [kik], [k2k])
        V(lambda e: e.tensor_copy(out=VA[:, va_tile, :, 0:128], in_=z[:, 256:512].rearrange("p (k d) -> p k d", k=2)), [zk], ["VA"])
        pb_ = banksb[b2]
        for kv in range(2):
            PE(lambda e, kv=kv: e.transpose(out=pb_[:, kv * 128:(kv + 1) * 128], in_=kr[:, kv * 128:(kv + 1) * 128], identity=identb[:]),
               [krk, "identb"], [PS(b2)])
        PE(lambda e: e.transpose(out=pb_[:, 256:384], in_=k2, identity=identb[:]), [k2k, "identb"], [PS(b2)])
        A(lambda e: e.copy(out=kt_dst, in_=pb_[:, 0:256].rearrange("p (k c) -> p k c", k=2)), [PS(b2)], [kt_key])
        A(lambda e: e.copy(out=kiT[:, kit_cols:kit_cols + 128], in_=pb_[:, 256:384]), [PS(b2)], ["kiT"])

    def kv_from_h(*a, **k):
        for _ in kv_from_h_gen(*a, **k):
            pass

    load_wkv()
    NPRE = 3
    pend_ = None
    for j in range(min(NPRE, NTILE)):
        LD(xres[:, j % 4, :], x_seq[j * 128:(j + 1) * 128, :], ["xres%d" % (j % 4)])
    for j in range(NTILE):
        bs = j % 2
        if j + NPRE < NTILE:
            LD(xres[:, (j + NPRE) % 4, :], x_seq[(j + NPRE) * 128:(j + NPRE + 1) * 128, :], ["xres%d" % ((j + NPRE) % 4)])
        hnb, hnk = (hn, "hn") if bs == 0 else (qrb, "qrb")
        hTb, hTk = (hTh[:, :, :], "hTh") if bs == 0 else (hT[:, :, 0:128], "hT")
        norm_tile(xres[:, j % 4, :], "xres%d" % (j % 4), 0, out_hn=hnb, okey=hnk, sc=bs)
        transpose_to(hTb, hTk, hnb, hnk, 8, bank=(2 if bs == 0 else 5))
        g_ = kv_from_h_gen(hTb, hTk, rope_seq[j], nk_p[j * 128:(j + 1) * 128, :], nv_p[j * 128:(j + 1) * 128, :],
                           nki_p[j * 128:(j + 1) * 128, :], j, KT[:, :, j * 128:(j + 1) * 128], "KT", j * 128, bs=bs)
        next(g_)
        if pend_ is not None:
            for _ in pend_:
                pass
        pend_ = g_
    for _ in pend_:
        pass
    V(lambda e: e.memset(sm[:, 12:13], 0.0), [], ["xres0", "xres1", "xres2", "xres3", "xres", "scores", "scoresB", "scoresC", "maskc", "maskcB", "amask", "qrb", "hT", "sm12"])

    maskflat = maskT[:, :, :].rearrange("p a b -> p (a b)")

    def indexer_chunk_epilogue(h, c0, w, bank):
        rb = rbuf[h % 2]
        rk = "rbuf%d" % (h % 2)
        A(lambda e: e.activation(out=rb[:, 0:w], in_=banks[bank][:, 0:w], func=AF.Relu, scale=absw[:, h:h + 1]),
          [PS(bank), "absw"], [rk])
        if h == 0:
            V(lambda e: e.tensor_scalar(out=scores[:, c0:c0 + w], in0=rb[:, 0:w], scalar1=sgnw[:, 0:1], scalar2=None, op0=ALU.mult),
              [rk, "sgnw"], ["scores"])
        else:
            V(lambda e: e.scalar_tensor_tensor(out=scores[:, c0:c0 + w], in0=rb[:, 0:w], scalar=sgnw[:, h:h + 1],
                                               in1=scores[:, c0:c0 + w], op0=ALU.mult, op1=ALU.add),
              [rk, "sgnw", "scores"], ["scores"])

    def score_bounds(L):
        lo, hi = sm[:, 4:5], sm[:, 5:6]
        V(lambda e: e.tensor_reduce(out=hi, in_=scores[:, 0:L], axis=AX.X, op=ALU.max), ["scores"], ["hi"])
        V(lambda e: e.tensor_reduce(out=lo, in_=scores[:, 0:L], axis=AX.X, op=ALU.min), ["scores"], ["lo"])
        V(lambda e: e.scalar_tensor_tensor(out=hi, in0=hi, scalar=2.0, in1=lo, op0=ALU.add, op1=ALU.subtract), ["hi", "lo"], ["hi"])
        V(lambda e: e.tensor_scalar(out=lo, in0=lo, scalar1=-1.0, scalar2=None, op0=ALU.add), ["lo"], ["lo"])
        V(lambda e: e.tensor_scalar(out=hk[:, 0:NBIS + 1], in0=pw[:, 0:NBIS + 1], scalar1=hi, scalar2=None, op0=ALU.mult), ["pw", "hi"], ["hk"])
        V(lambda e: e.tensor_scalar(out=hk[:, 32:32 + NBIS + 1], in0=hk[:, 0:NBIS + 1], scalar1=2.0, scalar2=None, op0=ALU.mult), ["hk"], ["hk"])

    def bisect(L):
        lo, mid, cnt, g = sm[:, 4:5], sm[:, 6:7], sm[:, 7:8], sm[:, 8:9]
        V(lambda e: e.tensor_tensor(out=mid, in0=lo, in1=hk[:, 0:1], op=ALU.add), ["lo", "hk"], ["mid"])
        for k in range(NBIS):
            V(lambda e: e.tensor_scalar(out=junk[:, 0:L], in0=scores[:, 0:L], scalar1=mid, scalar2=None, op0=ALU.is_ge, op1=ALU.add, accum_out=cnt),
              ["scores", "mid"], ["junk", "cnt"])
            V(lambda e, k=k: e.tensor_scalar(out=g, in0=cnt, scalar1=NSEL - 0.5, scalar2=hk[:, 32 + k + 1:32 + k + 2], op0=ALU.is_ge, op1=ALU.mult), ["cnt", "hk"], ["g"])
            V(lambda e, k=k: e.scalar_tensor_tensor(out=mid, in0=g, scalar=hk[:, k + 1:k + 2], in1=mid, op0=ALU.subtract, op1=ALU.add), ["g", "hk", "mid"], ["mid"])
        V(lambda e: e.tensor_tensor(out=lo, in0=mid, in1=hk[:, NBIS:NBIS + 1], op=ALU.subtract), ["mid", "hk"], ["lo"])

    def make_maskT(nkt, neg=False):
        lo = sm[:, 4:5]
        pb_ = banksb[2]
        for j0 in range(0, nkt, 8):
            n = min(8, nkt - j0)
            V(lambda e, j0=j0, n=n: e.tensor_scalar(out=maskc[:, 0:n * 128], in0=scores[:, j0 * 128:(j0 + n) * 128], scalar1=lo, scalar2=None, op0=ALU.is_ge),
              ["scores", "lo"], ["maskc"])
            for j in range(j0, j0 + n):
                PE(lambda e, j=j, j0=j0: e.transpose(out=pb_[:, (j - j0) * 128:(j - j0 + 1) * 128], in_=maskc[:, (j - j0) * 128:(j - j0 + 1) * 128], identity=identb[:]),
                   ["maskc", "identb"], [PS(2)])
            if neg:
                A(lambda e, j0=j0, n=n: e.activation(out=maskT[:, j0:j0 + n, :], in_=pb_[:, 0:n * 128].rearrange("p (k c) -> p k c", k=n),
                                                     func=AF.Identity, scale=30000.0, bias=sm[:, 11:12]), [PS(2), "negb"], ["maskT"])
            else:
                A(lambda e, j0=j0, n=n: e.copy(out=maskT[:, j0:j0 + n, :], in_=pb_[:, 0:n * 128].rearrange("p (k c) -> p k c", k=n)),
                  [PS(2)], ["maskT"])

    def topk_mask(L, nkt):
        bisect(L)
        make_maskT(nkt)

    wq_v = [None] * 4
    wq_key = [None] * 3

    def load_wq():
        for blk, off in enumerate((0, 512)):
            k, (v,) = load_w([(w_in[:, off:off + 512], 8, 512)])
            wq_v[blk] = v
            wq_key[blk] = k
        k, (v2, v3) = load_w([(w_in[:, O_QI:O_QI + 512], 8, 512), (w_in[:, O_WI:O_WI + 8], 8, 8)])
        wq_v[2], wq_v[3] = v2, v3
        wq_key[2] = k

    def q_proj_tile(hT_cols, rope_ap, qTd=None, qkey="qT0"):
        qTd = qT2[0] if qTd is None else qTd
        LD(rope_t[:], rope_ap, ["rope_t"])
        for blk in range(2):
            for kc in range(8):
                PE(lambda e, kc=kc, blk=blk: e.matmul(banks[blk][:, :], lhsT=hT[:, kc, hT_cols:hT_cols + 128], rhs=wq_v[blk][:, kc, :],
                                                      start=(kc == 0), stop=(kc == 7)), ["hT", wq_key[blk]], [PS(blk)])
            A(lambda e, blk=blk: e.copy(out=zs[:, blk * 512:(blk + 1) * 512], in_=banks[blk][:, :]), [PS(blk)], ["zs"])
        rope(zs[:, :], "zs", 8, 128, rope_t[:, 0:64], rope_t[:, 64:128], qrb[:], "qrb", eng="pool")
        transpose_to(qTd[:, :, :], qkey, qrb, "qrb", 8)
        for kc in range(8):
            PE(lambda e, kc=kc: e.matmul(banks[0][:, :], lhsT=hT[:, kc, hT_cols:hT_cols + 128], rhs=wq_v[2][:, kc, :], start=(kc == 0), stop=(kc == 7)),
               ["hT", wq_key[2]], [PS(0)])
        for kc in range(8):
            PE(lambda e, kc=kc: e.matmul(banks[1][:, 0:8], lhsT=hT[:, kc, hT_cols:hT_cols + 128], rhs=wq_v[3][:, kc, :], start=(kc == 0), stop=(kc == 7)),
               ["hT", wq_key[2]], [PS(1)])
        A(lambda e: e.copy(out=zs[:, 0:512], in_=banks[0][:, :]), [PS(0)], ["zs"])
        A(lambda e: e.copy(out=wi[:], in_=banks[1][:, 0:8]), [PS(1)], ["wi"])
        rope(zs[:, 0:512], "zs", 8, 64, rope_t[:, 128:160], rope_t[:, 160:192], qirb[:], "qirb", eng="pool")
        transpose_to(qiT[:, :, :], "qiT", qirb, "qirb", 4)
        A(lambda e: e.activation(out=absw[:], in_=wi[:], func=AF.Abs), ["wi"], ["absw"])
        A(lambda e: e.activation(out=sgnw[:], in_=wi[:], func=AF.Sign), ["wi"], ["sgnw"])

    o_raw = xt[:, :].rearrange("p (h d) -> p h d", h=8)

    IDXB = (2, 7)

    def gen_A(nkt):
        L = nkt * 128
        items = []
        for c0 in range(0, L, 512):
            w = min(512, L - c0)
            for h in range(8):
                items.append((c0, w, h))
        N = len(items)

        def mm(n):
            c0, w, h = items[n]
            bank = IDXB[n % 2]
            r0 = (h % 2) * 64
            PE(lambda e: e.matmul(banks[bank][:, 0:w], lhsT=qiT[r0:r0 + 64, h // 2, :], rhs=kiT[r0:r0 + 64, c0:c0 + w], start=True, stop=True),
               ["qiT", "kiT"], [PS(bank)])

        def relu(n):
            c0, w, h = items[n]
            bank = IDXB[n % 2]
            rb, rk = rbuf[n % 2], "rbuf%d" % (n % 2)
            A(lambda e: e.activation(out=rb[:, 0:w], in_=banks[bank][:, 0:w], func=AF.Relu, scale=absw[:, h:h + 1]), [PS(bank), "absw"], [rk])

        def acc(n):
            c0, w, h = items[n]
            rb, rk = rbuf[n % 2], "rbuf%d" % (n % 2)
            if h == 0:
                V(lambda e: e.tensor_scalar(out=scores[:, c0:c0 + w], in0=rb[:, 0:w], scalar1=sgnw[:, 0:1], scalar2=None, op0=ALU.mult), [rk, "sgnw"], ["scores"])
            else:
                V(lambda e: e.scalar_tensor_tensor(out=scores[:, c0:c0 + w], in0=rb[:, 0:w], scalar=sgnw[:, h:h + 1], in1=scores[:, c0:c0 + w],
                                                   op0=ALU.mult, op1=ALU.add), [rk, "sgnw", "scores"], ["scores"])

        for n in range(N + 2):
            if n < N:
                mm(n)
            if 0 <= n - 1 < N:
                relu(n - 1)
            if 0 <= n - 2 < N:
                acc(n - 2)
            yield
        score_bounds(L)
        V(lambda e: e.tensor_tensor(out=scores[:, L - 640:L], in0=scores[:, L - 640:L], in1=amask[:, :], op=ALU.add), ["scores", "amask"], ["scores"])
        yield
        lo, mid, cnt, g = sm[:, 4:5], sm[:, 6:7], sm[:, 7:8], sm[:, 8:9]
        V(lambda e: e.tensor_tensor(out=mid, in0=lo, in1=hk[:, 0:1], op=ALU.add), ["lo", "hk"], ["mid"])
        for k in range(NBIS):
            V(lambda e: e.tensor_scalar(out=junk[:, 0:L], in0=scores[:, 0:L], scalar1=mid, scalar2=None, op0=ALU.is_ge, op1=ALU.add, accum_out=cnt),
              ["scores", "mid"], ["junk", "cnt"])
            V(lambda e, k=k: e.tensor_scalar(out=g, in0=cnt, scalar1=NSEL - 0.5, scalar2=hk[:, 32 + k + 1:32 + k + 2], op0=ALU.is_ge, op1=ALU.mult), ["cnt", "hk"], ["g"])
            V(lambda e, k=k: e.scalar_tensor_tensor(out=mid, in0=g, scalar=hk[:, k + 1:k + 2], in1=mid, op0=ALU.subtract, op1=ALU.add), ["g", "hk", "mid"], ["mid"])
            yield
        V(lambda e: e.tensor_tensor(out=lo, in0=mid, in1=hk[:, NBIS:NBIS + 1], op=ALU.subtract), ["mid", "hk"], ["lo"])

    def n_items_A(nkt):
        return 8 * ((nkt * 128 + 511) // 512) + 3 + NBIS

    def stage_C(nkt, qTd, qkey, gen=None, gen_items=0):
        steps = [(kv, j) for kv in range(2) for j in range(nkt)]

        def emit_ST(i):
            kv, j = steps[i]
            bank = i % 2
            PE(lambda e, j=j, kv=kv, bank=bank: e.matmul(banks[bank][:, :], lhsT=KT[:, kv, j * 128:(j + 1) * 128],
                                                         rhs=qTd[:, kv * 4:(kv + 1) * 4, :], start=True, stop=False),
               ["KT", qkey], [PS(bank)])
            PE(lambda e, j=j, bank=bank: e.matmul(banks[bank][:, :].rearrange("p (g q) -> p g q", g=4), lhsT=identb[:],
                                                  rhs=maskT[:, j, :].unsqueeze(1).to_broadcast([128, 4, 128]), start=False, stop=True),
               ["identb", "maskT"], [PS(bank)])

        def emit_exp(i):
            bank = i % 2
            eb = ebuf[i % 3]
            ek = "ebuf%d" % (i % 3)
            A(lambda e: e.activation(out=eb[:], in_=banks[bank][:, :], func=AF.Exp, scale=128.0 ** -0.5), [PS(bank)], [ek])

        emit_ST(0)
        if len(steps) > 1:
            emit_ST(1)
        emit_exp(0)
        acc = 0.0
        rate = max(1.0, 4.5 * gen_items / float(len(steps))) if gen is not None else 0.0
        for i, (kv, j) in enumerate(steps):
            if i + 2 < len(steps):
                emit_ST(i + 2)
            if i + 1 < len(steps):
                emit_exp(i + 1)
            eb = ebuf[i % 3]
            ek = "ebuf%d" % (i % 3)
            if gen is not None:
                acc += rate
                while acc >= 1.0:
                    acc -= 1.0
                    next(gen, None)
            for g in range(4):
                PE(lambda e, j=j, kv=kv, g=g, eb=eb: e.matmul(banks[3 + g][:, 0:129], lhsT=eb[:, g * 128:(g + 1) * 128], rhs=VA[:, j, kv, :],
                                                              start=(j == 0), stop=(j == nkt - 1)), [ek, "VA"], [PS(3 + g)])
            if j == nkt - 1:
                for g in range(4):
                    hh = kv * 4 + g
                    A(lambda e, g=g, hh=hh: e.copy(out=o_raw[:, hh, :], in_=banks[3 + g][:, 0:128]), [PS(3 + g)], ["xt"])
                    A(lambda e, g=g, hh=hh: e.copy(out=den8[:, hh:hh + 1], in_=banks[3 + g][:, 128:129]), [PS(3 + g)], ["den8"])
        if gen is not None:
            for _ in gen:
                pass

    def stage_D(oT_cols):
        V(lambda e: e.reciprocal(out=den8[:, 8:16], in_=den8[:, 0:8]), ["den8"], ["rden8"])
        V(lambda e: e.tensor_tensor(out=on_bt[:, :, :], in0=o_raw, in1=den8[:, 8:16].unsqueeze(2).to_broadcast([128, 8, 128]), op=ALU.mult),
          ["xt", "rden8"], ["on_bt"])
        pb_ = banksb[2]
        for hh in range(8):
            PE(lambda e, hh=hh: e.transpose(out=pb_[:, hh * 128:(hh + 1) * 128], in_=on_bt[:, hh, :], identity=identb[:]), ["on_bt", "identb"], [PS(2)])
        A(lambda e: e.copy(out=oT[:, :, oT_cols:oT_cols + 128], in_=pb_[:, :].rearrange("p (k c) -> p k c", k=8)), [PS(2)], ["oT"])

    def dense_tail(NT, ntl, sample, x_dram_out, p_dram, slot_is_last):
        NB, WW = (16, 8) if sample else (1, NT)
        UW = NB * (WW + 2)

        def uview(buf, f):
            return buf[:, f, 0:UW].rearrange("p (b w) -> p b w", b=NB)

        def pview(bank):
            return banks[bank][:, 0:NT].rearrange("p (b w) -> p b w", b=NB)

        def v3(ap):
            return ap.rearrange("p (b w) -> p b w", b=NB)

        for qb in range(4):
            segs = [("cg", O_CG), ("xc", O_XC), ("bg", O_BG), ("gb", O_GB), ("ga", O_GA)]
            for si, (sname, soff) in enumerate(segs):
                wk, (wv,) = load_w([(w_in[:, soff + qb * 256: soff + qb * 256 + 256], 8, 256)])
                for f in range(2):
                    fc = qb * 2 + f
                    bank = f % 2
                    for kc in range(8):
                        PE(lambda e, kc=kc, f=f, bank=bank: e.matmul(banks[bank][:, 0:NT], lhsT=wv[:, kc, f * 128:(f + 1) * 128], rhs=hT[:, kc, 0:NT],
                                                                     start=(kc == 0), stop=(kc == 7)), ["hT", wk], [PS(bank)])
                    if sname in ("cg", "xc") and not sample:
                        for kc in range(8):
                            PE(lambda e, kc=kc, f=f: e.matmul(banks[7][:, 0:2], lhsT=wv[:, kc, f * 128:(f + 1) * 128], rhs=hTh[:, kc, 0:2],
                                                              start=(kc == 0), stop=(kc == 7)), ["hTh", wk], [PS(7)])
                    cv_ = uview(CGC, f)[:, :, 0:WW]
                    if sname == "cg":
                        A(lambda e, f=f, bank=bank: e.copy(out=uview(CGC, f)[:, :, 2:WW + 2], in_=pview(bank)), [PS(bank)], ["CGC"])
                        if not sample:
                            A(lambda e, f=f: e.copy(out=CGC[:, f, 0:2], in_=banks[7][:, 0:2]), [PS(7)], ["CGC"])
                    elif sname == "xc":
                        V(lambda e, f=f, bank=bank: e.tensor_tensor(out=uview(U, f)[:, :, 2:WW + 2], in0=pview(bank), in1=uview(CGC, f)[:, :, 2:WW + 2], op=ALU.mult),
                          [PS(bank), "CGC"], ["U"])
                        if not sample:
                            V(lambda e, f=f: e.tensor_tensor(out=U[:, f, 0:2], in0=banks[7][:, 0:2], in1=CGC[:, f, 0:2], op=ALU.mult), [PS(7), "CGC"], ["U"])
                        else:
                            V(lambda e, f=f, fc=fc: e.tensor_copy(out=uview(U, f)[:, :, 0:2], in_=ulast[:, fc, :].rearrange("p (b w) -> p b w", b=16)),
                              ["ulast"], ["U"])
                        if sample:
                            V(lambda e, f=f, fc=fc: e.tensor_copy(out=ulast[:, fc, :].rearrange("p (b w) -> p b w", b=16), in_=uview(U, f)[:, :, WW:WW + 2]),
                              ["U"], ["ulast"])
                        elif slot_is_last:
                            V(lambda e, f=f, fc=fc: e.tensor_copy(out=ulast[:, fc, 0:2], in_=U[:, f, NT:NT + 2]), ["U"], ["ulast"])
                        V(lambda e, f=f, fc=fc, cv_=cv_: e.tensor_scalar(out=cv_, in0=uview(U, f)[:, :, 0:WW], scalar1=cw[:, fc, 0:1], scalar2=None, op0=ALU.mult),
                          ["U", "cw", "CGC"], ["CGC"])
                        for tap in (1, 2):
                            V(lambda e, f=f, fc=fc, cv_=cv_, tap=tap: e.scalar_tensor_tensor(out=cv_, in0=uview(U, f)[:, :, tap:tap + WW], scalar=cw[:, fc, tap:tap + 1],
                                                                                             in1=cv_, op0=ALU.mult, op1=ALU.add), ["U", "cw", "CGC"], ["CGC"])
                    elif sname == "bg":
                        V(lambda e, cv_=cv_, bank=bank: e.tensor_tensor(out=cv_, in0=pview(bank), in1=cv_, op=ALU.mult), [PS(bank), "CGC"], ["CGC"])
                    elif sname == "gb":
                        A(lambda e, bank=bank: e.activation(out=SG[:, 0:NT], in_=banks[bank][:, 0:NT], func=AF.Sigmoid), [PS(bank)], ["t2"])
                        V(lambda e, cv_=cv_: e.tensor_tensor(out=cv_, in0=cv_, in1=v3(SG[:, 0:NT]), op=ALU.mult), ["t2", "CGC"], ["CGC"])
                    else:
                        A(lambda e, bank=bank: e.activation(out=SG[:, 0:NT], in_=banks[bank][:, 0:NT], func=AF.Sigmoid), [PS(bank)], ["t2"])
                        V(lambda e, fc=fc: e.tensor_tensor(out=T3[:, 0:NT], in0=SG[:, 0:NT], in1=oT[:, fc, 0:NT], op=ALU.mult), ["t2", "oT"], ["t1"])
                        V(lambda e, fc=fc, cv_=cv_: e.tensor_tensor(out=v3(oT[:, fc, 0:NT]), in0=v3(T3[:, 0:NT]), in1=cv_, op=ALU.add), ["t1", "CGC"], ["oT"])
        if sample or slot_is_last:
            ncol = 32 if sample else 2
            for fc in range(8):
                PE(lambda e, fc=fc: e.transpose(out=banks[6 + fc // 4][0:ncol, (fc % 4) * 128:(fc % 4) * 128 + 128],
                                                in_=ulast[:, fc, 0:ncol], identity=ident32[:, :]), ["ulast", "ident32"], [PS(6 + fc // 4)])
            for half in range(2):
                A(lambda e, half=half: e.copy(out=zs[0:ncol, half * 512:(half + 1) * 512], in_=banks[6 + half][0:ncol, :]), [PS(6 + half)], ["zs"])
            ST((nconv_s if sample else nconv_p)[:, :], zs[0:ncol, :], ["zs"])
        wo = []
        for blk in range(2):
            wk, (wv,) = load_w([(w_out[:, blk * 512:(blk + 1) * 512], 8, 512)])
            wo.append((wv, wk))
        for t in range(ntl):
            for blk in range(2):
                wv, wk = wo[blk]
                for kc in range(8):
                    PE(lambda e, kc=kc, t=t, blk=blk, wv=wv: e.matmul(banks[blk][:, :], lhsT=oT[:, kc, t * 128:(t + 1) * 128], rhs=wv[:, kc, :], start=(kc == 0), stop=(kc == 7)),
                       ["oT", wk], [PS(blk)])
                V(lambda e, t=t, blk=blk: e.tensor_tensor(out=xres[:, t, blk * 512:(blk + 1) * 512], in0=banks[blk][:, :], in1=xres[:, t, blk * 512:(blk + 1) * 512], op=ALU.add),
                  [PS(blk), "xres"], ["xres"])
        norm_T_tiles([(xres[:, t, :], "xres", 1, hT[:, :, t * 128:(t + 1) * 128], "hT") for t in range(ntl)])
        for fh in range(2):
            for (i0, n) in ((0, 2), (2, 2), (4, 2), (6, 2), (8, 2), (10, 1)):
                ci = fh * 11 + i0
                wk, (wg, wu) = load_w([(w_gu[:, ci * 128:(ci + n) * 128], 8, n * 128), (w_gu[:, DFF + ci * 128: DFF + (ci + n) * 128], 8, n * 128)])
                for f in range(n):
                    for kc in range(8):
                        PE(lambda e, kc=kc, f=f, wg=wg: e.matmul(banks[0][:, 0:NT], lhsT=wg[:, kc, f * 128:(f + 1) * 128], rhs=hT[:, kc, 0:NT], start=(kc == 0), stop=(kc == 7)),
                           ["hT", wk], [PS(0)])
                    for kc in range(8):
                        PE(lambda e, kc=kc, f=f, wu=wu: e.matmul(banks[1][:, 0:NT], lhsT=wu[:, kc, f * 128:(f + 1) * 128], rhs=hT[:, kc, 0:NT], start=(kc == 0), stop=(kc == 7)),
                           ["hT", wk], [PS(1)])
                    A(lambda e: e.activation(out=SG[:, 0:NT], in_=banks[0][:, 0:NT], func=AF.Silu), [PS(0)], ["t2"])
                    V(lambda e, i0=i0, f=f: e.tensor_tensor(out=actT[:, i0 + f, 0:NT], in0=banks[1][:, 0:NT], in1=SG[:, 0:NT], op=ALU.mult), [PS(1), "t2"], ["actT"])
            for cb in range(4):
                wk, (wv,) = load_w([(w_dn[fh * 1408:(fh + 1) * 1408, cb * 256:(cb + 1) * 256], 11, 256)])
                for t in range(ntl):
                    bank = t % 2
                    for i in range(11):
                        PE(lambda e, i=i, t=t, bank=bank, wv=wv: e.matmul(banks[bank][:, 0:256], lhsT=actT[:, i, t * 128:(t + 1) * 128], rhs=wv[:, i, :], start=(i == 0), stop=(i == 10)),
                           ["actT", wk], [PS(bank)])
                    V(lambda e, t=t, cb=cb, bank=bank: e.tensor_tensor(out=xres[:, t, cb * 256:(cb + 1) * 256], in0=banks[bank][:, 0:256], in1=xres[:, t, cb * 256:(cb + 1) * 256], op=ALU.add),
                      [PS(bank), "xres"], ["xres"])
        norm_T_tiles([(xres[:, t, :], "xres", 2, hT[:, :, t * 128:(t + 1) * 128], "hT") for t in range(ntl)])
        for t in range(ntl):
            LD(pt_f[:], p_dram[t * 128:(t + 1) * 128, :], ["pt_f"])
            A(lambda e: e.copy(out=pb[:], in_=pt_f[:]), ["pt_f"], ["pb"])
            transpose_to(pT[:, :, t * 128:(t + 1) * 128], "pT", pb, "pb", 2)
        for blk in range(2):
            wk, (wgv, wpv) = load_w([(w_pg[:, blk * 512:(blk + 1) * 512], 8, 512), (w_ple[:, blk * 512:(blk + 1) * 512], 2, 512)])
            for t in range(ntl):
                for kc in range(8):
                    PE(lambda e, kc=kc, t=t, wgv=wgv: e.matmul(banks[0][:, :], lhsT=hT[:, kc, t * 128:(t + 1) * 128], rhs=wgv[:, kc, :], start=(kc == 0), stop=(kc == 7)),
                       ["hT", wk], [PS(0)])
                for kc in range(2):
                    PE(lambda e, kc=kc, t=t, wpv=wpv: e.matmul(banks[1][:, :], lhsT=pT[:, kc, t * 128:(t + 1) * 128], rhs=wpv[:, kc, :], start=(kc == 0), stop=(kc == 1)),
                       ["pT", wk], [PS(1)])
                A(lambda e: e.activation(out=SG[:, :], in_=banks[0][:, :], func=AF.Sigmoid), [PS(0)], ["t2"])
                V(lambda e: e.tensor_tensor(out=T3[:, :], in0=banks[1][:, :], in1=SG[:, :], op=ALU.mult), [PS(1), "t2"], ["t1"])
                V(lambda e, t=t, blk=blk: e.tensor_tensor(out=xres[:, t, blk * 512:(blk + 1) * 512], in0=T3[:, :], in1=xres[:, t, blk * 512:(blk + 1) * 512], op=ALU.add),
                  ["t1", "xres"], ["xres"])
        for t in range(ntl):
            yb, yk = (zs[:, :], "zs") if t % 2 == 0 else (scores[:, 0:1024], "scores")
            norm_tile(xres[:, t, :], "xres", 3, out_hn=yb, okey=yk, sc=t % 2)
            ST(x_dram_out[t * 128:(t + 1) * 128, :], yb, [yk])

    for s in range(4):
        kmax = 4 * max(OWN_A[s], OWN_B[s])
        LD(xt[:], x_halo[s], ["xt"])
        for t in range(4):
            LD(xres[:, t, :], x_own[(s * 4 + t) * 128:(s * 4 + t + 1) * 128, :], ["xres"])
        norm_T_tiles([(xt[:], "xt", 0, hTh[:, :, :], "hTh")] +
                     [(xres[:, t, :], "xres", 0, hT[:, :, t * 128:(t + 1) * 128], "hT") for t in range(4)])
        load_wq()

        def prep(t):
            LD(amask[:], amask_d[s * 4 + t], ["amask"])
            q_proj_tile(t * 128, rope_own[s * 4 + t], qT2[t % 2], "qT%d" % (t % 2))

        prep(0)
        for _ in gen_A(kmax + 1):
            pass
        make_maskT(kmax + 1, neg=True)
        prep(1)
        for t in range(4):
            gen, gi = None, 0
            if t + 1 < 4:
                gen, gi = gen_A(kmax + t + 2), n_items_A(kmax + t + 2)
            stage_C(kmax + t + 1, qT2[t % 2], "qT%d" % (t % 2), gen, gi)
            if t + 2 < 4:
                prep(t + 2)
            stage_D(t * 128)
            if t + 1 < 4:
                make_maskT(kmax + t + 2, neg=True)
        dense_tail(512, 4, False, y_p[s * 512:(s + 1) * 512, :], p_own[s * 512:(s + 1) * 512, :], s == 3)

    stc = xres[0:32, 1, :]
    kg = WS[0][:, 0:4096].rearrange("p (a c) -> p a c", a=16)
    vgs = [WS[1][:, 0:4096].rearrange("p (a c) -> p a c", a=16), WS[2][:, 0:4096].rearrange("p (a c) -> p a c", a=16)]
    vgk = ["WS1", "WS2"]
    qiT8 = actT[0:64, 0:2, :].rearrange("p a (h c) -> p (a h) c", c=128)
    kis = [actT[:, 2 + 2 * i:4 + 2 * i, :].rearrange("p a (j c) -> p (a j) c", c=64) for i in range(2)]
    kiTb = kiT[0:64, 0:2048].rearrange("p (j c) -> p j c", c=128)
    STall = xres[:, 2:4, :].rearrange("p a (j c) -> p (a j) c", c=128)
    wiB = xt[:, :]
    rl = zs[:, :]
    ptab_f = t1[:, 0:32]
    wi_scr = nc.dram_tensor("wi_scr", [128, 8], F32).ap()

    LD(xres[:, 0, :], x_smp[:, :], ["xres"])
    norm_tile(xres[:, 0, :], "xres", 0)
    transpose_to(hT[:, :, 0:128], "hT", hn, "hn", 8)
    LD(stc, st_conv[:, :], ["xres1"])
    for fc in range(8):
        PE(lambda e, fc=fc: e.transpose(out=banks[6 + fc // 4][:, (fc % 4) * 32:(fc % 4) * 32 + 32], in_=xres[0:32, 1, fc * 128:(fc + 1) * 128], identity=ident32[0:32, 0:32]),
           ["xres1", "ident32"], [PS(6 + fc // 4)])
    for half in range(2):
        A(lambda e, half=half: e.copy(out=ulast[:, half * 4:(half + 1) * 4, :], in_=banks[6 + half][:, 0:128].rearrange("p (k c) -> p k c", k=4)),
          [PS(6 + half)], ["ulast"])
    load_wkv()
    kv_from_h(hT[:, :, 0:128], "hT", rope_own[16], nk_s[:, :], nv_s[:, :], nki_s[:, :], 16, kTn[:, :, :], "kTn", 2048)
    load_wq()
    LD(amask[:], amask_d[16], ["amask"])
    q_proj_tile(0, rope_own[16], qT2[0], "qT0")
    pb_ = banksb[2]
    for h in range(8):
        PE(lambda e, h=h: e.transpose(out=pb_[0:64, h * 128:(h + 1) * 128], in_=qirb[:, h * 64:(h + 1) * 64], identity=identb[:]), ["qirb", "identb"], [PS(2)])
    A(lambda e: e.copy(out=qiT8, in_=pb_[0:64, :].rearrange("p (h c) -> p h c", h=8)), [PS(2)], ["actT"])
    B.dma("sp", wi_scr, wi[:, :], ["wi"], ["wi_scr"])
    B.dma("sp", wiB, wi_scr.rearrange("a h -> (a h)").partition_broadcast(128), ["wi_scr"], ["xt"])
    wiBv = wiB.rearrange("p (b t h) -> p b h t", b=16, t=8)
    LD(ptab_i[:], ptab[:, :], ["ptab_i"])
    V(lambda e: e.tensor_copy(out=ptab_f, in_=ptab_i[:]), ["ptab_i"], ["t1"])
    V(lambda e: e.tensor_scalar(out=ptab_f, in0=ptab_f, scalar1=16.0, scalar2=piota[:, 0:1], op0=ALU.mult, op1=ALU.add), ["t1", "piota"], ["t1"])
    V(lambda e: e.tensor_copy(out=gidx[:], in_=ptab_f), ["t1"], ["gidx"])

    def gather_ki(b):
        for half in range(2):
            B.dma("pool", kis[b % 2][:, half * 8:(half + 1) * 8, :].rearrange("p a c -> p (a c)"), cki, ["gidx"], ["kis%d" % (b % 2)], indirect=gidx[:, b * 2 + half:b * 2 + half + 1])

    gather_ki(0)
    for b in range(16):
        if b + 1 < 16:
            gather_ki(b + 1)
        kb, kk = kis[b % 2], "kis%d" % (b % 2)
        for g0 in (0, 8):
            for k in range(8):
                PE(lambda e, g0=g0, k=k, kb=kb: e.transpose(out=pb_[0:64, k * 128:(k + 1) * 128], in_=kb[:, g0 + k, :], identity=identb[:]), [kk, "identb"], [PS(2)])
            A(lambda e, g0=g0: e.copy(out=kiTb[:, g0:g0 + 8, :], in_=pb_[0:64, :].rearrange("p (k c) -> p k c", k=8)), [PS(2)], ["kiT"])
        for j in range(16):
            PE(lambda e, j=j, b=b: e.matmul(banks[j // 8][:, (j % 8) * 64:(j % 8) * 64 + 64].rearrange("p (h t) -> p h t", h=8), lhsT=kiTb[:, j, :],
                                            rhs=qiT8[:, :, b * 8:(b + 1) * 8], start=True, stop=True), ["kiT", "actT"], [PS(j // 8)])
        for bk in range(2):
            A(lambda e, bk=bk: e.activation(out=rl[:, bk * 512:(bk + 1) * 512], in_=banks[bk][:, :], func=AF.Relu), [PS(bk)], ["zs"])
        V(lambda e, b=b: e.tensor_tensor(out=rl.rearrange("p (j h t) -> p j h t", j=16, h=8), in0=rl.rearrange("p (j h t) -> p j h t", j=16, h=8),
                                         in1=wiBv[:, b, :, :].unsqueeze(1).to_broadcast([128, 16, 8, 8]), op=ALU.mult), ["zs", "xt"], ["zs"])
        V(lambda e, b=b: e.tensor_reduce(out=STall[:, :, b * 8:(b + 1) * 8], in_=rl.rearrange("p (j h t) -> p j t h", j=16, h=8), axis=AX.X, op=ALU.add),
          ["zs"], ["xres23"])
    for j0 in range(0, 16, 4):
        for k in range(4):
            PE(lambda e, j0=j0, k=k: e.transpose(out=banks[j0 // 4 % 2][:, k * 128:(k + 1) * 128], in_=STall[:, j0 + k, :], identity=ident32[:, :]),
               ["xres23", "ident32"], [PS(j0 // 4 % 2)])
        A(lambda e, j0=j0: e.copy(out=scores[:, j0 * 128:(j0 + 4) * 128], in_=banks[j0 // 4 % 2][:, :]), [PS(j0 // 4 % 2)], ["scores"])
    V(lambda e: e.memset(sm[:, 12:13], 0.0), [], ["kis0", "kis1", "actT", "sm12"])
    for h in range(8):
        bank = h % 2
        r0 = (h % 2) * 64
        PE(lambda e, h=h, r0=r0, bank=bank: e.matmul(banks[bank][:, 0:128], lhsT=qiT[r0:r0 + 64, h // 2, :], rhs=kiT[r0:r0 + 64, 2048:2176], start=True, stop=True),
           ["qiT", "kiT"], [PS(bank)])
        indexer_chunk_epilogue(h, 2048, 128, bank)
    score_bounds(2176)
    V(lambda e: e.tensor_tensor(out=scores[:, 2048:2176], in0=scores[:, 2048:2176], in1=amask[:, 0:128], op=ALU.add), ["scores", "amask"], ["scores"])
    topk_mask(2176, 17)

    def gather_kv(b):
        for half in range(2):
            B.dma("pool", kg[:, half * 8:(half + 1) * 8, :].rearrange("p a c -> p (a c)"), ck, ["gidx"], ["WS0"], indirect=gidx[:, b * 2 + half:b * 2 + half + 1])
        for half in range(2):
            B.dma("pool", vgs[b % 2][:, half * 8:(half + 1) * 8, :].rearrange("p a c -> p (a c)"), cv, ["gidx"], [vgk[b % 2]], indirect=gidx[:, b * 2 + half:b * 2 + half + 1])

    gather_kv(0)
    for b in range(16):
        vg, vk = vgs[b % 2], vgk[b % 2]
        for g0 in range(0, 32, 8):
            for k in range(8):
                pg, kv = (g0 + k) // 2, (g0 + k) % 2
                PE(lambda e, k=k, pg=pg, kv=kv: e.transpose(out=pb_[:, k * 128:(k + 1) * 128], in_=kg[:, pg, kv * 128:(kv + 1) * 128], identity=identb[:]),
                   ["WS0", "identb"], [PS(2)])
            A(lambda e, g0=g0: e.copy(out=KT[:, :, (g0 // 2) * 128:(g0 // 2) * 128 + 512].rearrange("p k (g c) -> p g k c", g=4),
                                      in_=pb_[:, :].rearrange("p (g k c) -> p g k c", g=4, k=2)), [PS(2)], ["KT"])
        if b + 1 < 16:
            gather_kv(b + 1)
        for j in range(17):
            bank = 3 + j // 8
            for kv in range(2):
                c0 = (j % 8) * 64 + kv * 32
                lhs = KT[:, kv, j * 128:(j + 1) * 128] if j < 16 else kTn[:, kv, :]
                PE(lambda e, kv=kv, bank=bank, c0=c0, b=b, lhs=lhs: e.matmul(banks[bank][:, c0:c0 + 32].rearrange("p (g t) -> p g t", g=4), lhsT=lhs,
                                                                           rhs=qT[:, kv * 4:(kv + 1) * 4, b * 8:(b + 1) * 8], start=True, stop=True),
                   ["KT", "kTn", "qT0"], [PS(bank)])
        for bk in range(3):
            n = 8 if bk < 2 else 1
            A(lambda e, bk=bk, n=n: e.activation(out=eS[:, bk * 8:bk * 8 + n, :], in_=banks[3 + bk][:, 0:n * 64].rearrange("p (j c) -> p j c", j=n), func=AF.Exp, scale=128.0 ** -0.5),
              [PS(3 + bk)], ["eS"])
        V(lambda e, b=b: e.tensor_tensor(out=eS[:, :, :].rearrange("p j (h t) -> p j h t", h=8), in0=eS[:, :, :].rearrange("p j (h t) -> p j h t", h=8),
                                         in1=maskT[:, 0:17, b * 8:(b + 1) * 8].unsqueeze(2).to_broadcast([128, 17, 8, 8]), op=ALU.mult), ["eS", "maskT"], ["eS"])
        for kv in range(2):
            for j in range(17):
                vl = vg[:, j, kv * 128:(kv + 1) * 128] if j < 16 else VA[:, 16, kv, 0:128]
                PE(lambda e, j=j, kv=kv, vl=vl: e.matmul(banks[6][:, kv * 32:(kv + 1) * 32], lhsT=vl, rhs=eS[:, j, kv * 32:(kv + 1) * 32], start=(j == 0), stop=(j == 16)),
                   ["VA", vk, "eS"], [PS(6)])
        for kv in range(2):
            for j in range(17):
                PE(lambda e, j=j, kv=kv: e.matmul(banks[7][:, kv * 32:(kv + 1) * 32], lhsT=onesb[:, :], rhs=eS[:, j, kv * 32:(kv + 1) * 32], start=(j == 0), stop=(j == 16)),
                   ["onesb", "eS"], [PS(7)])
        V(lambda e: e.reciprocal(out=rden[:, :], in_=banks[7][:, 0:64]), [PS(7)], ["rden"])
        V(lambda e, b=b: e.tensor_tensor(out=oT[:, :, b * 8:(b + 1) * 8], in0=banks[6][:, 0:64].rearrange("p (h t) -> p h t", h=8),
                                         in1=rden[:, :].rearrange("p (h t) -> p h t", h=8), op=ALU.mult), [PS(6), "rden"], ["oT"])
    if debug:
        dbg1 = dout("dbg1", [128, D])
        dbg2 = dout("dbg2", [128, 2176])
        dbg3 = dout("dbg3", [128, 16])
        V(lambda e: e.tensor_copy(out=zs[:, :].rearrange("p (h c) -> p h c", h=8), in_=oT[:, :, 0:128]), ["oT"], ["zs"])
        ST(dbg1, zs[:, :], ["zs"])
        ST(dbg2, scores[:, 0:2176], ["scores"])
        ST(dbg3, sm[:, :], ["lo", "hi", "cnt"])
        dbgK = dout("dbgK", [128, 2, 2048], BF16)
        dbgV = dout("dbgV", [128, 17, 2, 129], BF16)
        dbgE = dout("dbgE", [128, 17, 64], BF16)
        dbgM = dout("dbgM", [128, 17, 128], BF16)
        dbgQ = dout("dbgQ", [128, 8, 128], BF16)
        ST(dbgK, KT[:, :, 0:2048], ["KT"])
        ST(dbgV, VA[:, 0:17, :, :], ["VA"])
        ST(dbgE, eS[:, :, :], ["eS"])
        ST(dbgM, maskT[:, 0:17, :], ["maskT"])
        ST(dbgQ, qT[:, :, :], ["qT0"])
    dense_tail(128, 1, True, y_s, p_smp, False)

    B.finish()
    return nc


_CACHE = {}


def _rope_tab(pos):
    pos = np.asarray(pos, np.float32)
    out = np.zeros((pos.shape[0], 192), np.float32)
    for (half, o) in ((64, 0), (32, 128)):
        freqs = (np.float32(10000.0) ** (-np.arange(half, dtype=np.float32) / np.float32(half))).astype(np.float32)
        ang = pos[:, None] * freqs[None, :]
        out[:, o:o + half] = np.cos(ang)
        out[:, o + half:o + 2 * half] = np.sin(ang)
    return out


def kernel(x_prompt, x_sample, cache_k, cache_v, cache_kidx, state_conv, page_table,
           p_prompt, p_sample, norm_mix, w_in, conv_w, w_out, norm_ffn, w_gate_up, w_down,
           norm_ple, w_ple, w_ple_gate, norm_final):
    f = np.float32
    if "nc" not in _CACHE:
        _CACHE["nc"] = build_program(debug=bool(_CACHE.get("debug")))
    nc = _CACHE["nc"]
    ck = np.ascontiguousarray(np.asarray(cache_k, f).reshape(NPHYS * 16, 2048))
    cv = np.ascontiguousarray(np.asarray(cache_v, f).reshape(NPHYS * 16, 2048))
    cki = np.ascontiguousarray(np.asarray(cache_kidx, f).reshape(NPHYS * 16, 512))
    gains = np.stack([np.broadcast_to(np.asarray(g, f).reshape(1, D), (128, D)) for g in (norm_mix, norm_ffn, norm_ple, norm_final)]).astype(f)
    convw = np.ascontiguousarray(np.asarray(conv_w, f).reshape(3, 8, 128).transpose(2, 1, 0))
    ident = np.eye(128, dtype=f)
    rope_seq = np.stack([_rope_tab(np.arange(j * 128, (j + 1) * 128)) for j in range(NTILE)])
    piota = (np.arange(128) % 16).astype(f).reshape(128, 1)
    shared = dict(ck=ck, cv=cv, cki=cki, w_in=np.asarray(w_in, f)[0], w_out=np.asarray(w_out, f)[0], w_gu=np.asarray(w_gate_up, f)[0],
                  w_dn=np.asarray(w_down, f)[0], w_pg=np.asarray(w_ple_gate, f)[0], w_ple=np.asarray(w_ple, f)[0], gains=gains, convw=convw,
                  ident=ident, rope_seq=rope_seq, piota=piota)
    in_maps = []
    qpos_l = np.arange(128)
    for c in range(8):
        bseq, par = c // 2, c % 2
        own = OWN_A if par == 0 else OWN_B
        xs = np.asarray(x_prompt, f)[bseq]
        ps = np.asarray(p_prompt, f)[0, bseq]
        x_own = np.concatenate([xs[s * 512:(s + 1) * 512] for s in own])
        p_own = np.concatenate([ps[s * 512:(s + 1) * 512] for s in own])
        x_halo = np.zeros((4, 128, D), f)
        for i, s in enumerate(own):
            if s > 0:
                x_halo[i, 0:2] = xs[s * 512 - 2:s * 512]
        rope_own = np.zeros((17, 128, 192), f)
        amask = np.zeros((17, 128, 640), f)
        for i, s in enumerate(own):
            kmax = 4 * max(OWN_A[i], OWN_B[i])
            for t in range(4):
                qpos = (s * 4 + t) * 128 + qpos_l
                rope_own[i * 4 + t] = _rope_tab(qpos)
                nkt = kmax + t + 1
                kpos = np.arange((nkt - 5) * 128, nkt * 128)
                amask[i * 4 + t] = np.where(kpos[None, :] <= qpos[:, None], 0.0, -BIG)
        rope_own[16] = _rope_tab(2048 + (qpos_l % 8))
        bq, tq = qpos_l // 8, qpos_l % 8
        amask[16, :, 0:128] = np.where((bq[None, :] == bq[:, None]) & (tq[None, :] <= tq[:, None]), 0.0, -BIG)
        ptc = np.asarray(page_table, np.int32)[c * 16:(c + 1) * 16]
        pl = np.arange(128) // 16
        pt = np.stack([ptc[bb, hf * 8 + pl] for bb in range(16) for hf in range(2)], axis=1).astype(np.int32)
        m = dict(shared)
        m.update(x_seq=xs, x_own=x_own, x_halo=x_halo, p_own=p_own,
                 x_smp=np.asarray(x_sample, f)[c * 16:(c + 1) * 16].reshape(128, D),
                 p_smp=np.asarray(p_sample, f)[0, c * 16:(c + 1) * 16].reshape(128, 256),
                 st_conv=np.asarray(state_conv, f)[0, c * 16:(c + 1) * 16].reshape(32, D),
                 ptab=np.ascontiguousarray(pt),
                 rope_own=rope_own, amask=amask)
        in_maps.append({k: np.ascontiguousarray(v) for k, v in m.items()})
    res = run_bass_kernel_spmd(nc, in_maps, core_ids=list(range(8)))
    R = res.results
    _CACHE['R'] = R
    y_prompt = np.zeros((4, SEQ, D), f)
    for c in range(8):
        own = OWN_A if c % 2 == 0 else OWN_B
        for i, s in enumerate(own):
            y_prompt[c // 2, s * 512:(s + 1) * 512] = R[c]["y_p"][i * 512:(i + 1) * 512]
    y_sample = np.concatenate([R[c]["y_s"].reshape(16, 8, D) for c in range(8)], 0)
    nk_p = np.stack([R[2 * b]["nk_p"].reshape(SEQ, 2, 128) for b in range(4)])[None]
    nv_p = np.stack([R[2 * b]["nv_p"].reshape(SEQ, 2, 128) for b in range(4)])[None]
    nki_p = np.stack([R[2 * b]["nki_p"] for b in range(4)])[None]
    nconv_p = np.stack([R[2 * b]["nconv_p"] for b in range(4)])[None]
    nk_s = np.concatenate([R[c]["nk_s"].reshape(16, 8, 2, 128) for c in range(8)], 0)[None]
    nv_s = np.concatenate([R[c]["nv_s"].reshape(16, 8, 2, 128) for c in range(8)], 0)[None]
    nki_s = np.concatenate([R[c]["nki_s"].reshape(16, 8, 64) for c in range(8)], 0)[None]
    nconv_s = np.concatenate([R[c]["nconv_s"].reshape(16, 2, D) for c in range(8)], 0)[None]
    outs = (y_prompt, y_sample, nk_p, nv_p, nki_p, nconv_p, nk_s, nv_s, nki_s, nconv_s)
    return tuple(np.ascontiguousarray(o, dtype=f) for o in outs)
```
